# Optimizing a Trainium2 kernel written in Bass

```python
import jax, jax.numpy as jnp
from jax import lax
import numpy as np

D_MODEL = 4096
BATCH = 4
SEQ = 4096
DEPTH = 2
DEC_BATCH = 4
DEC_SEQ = 2048
PAST_LEN = 128

HEAD_DIM = 128
D_MIX = D_MODEL
N_HEADS = D_MIX // HEAD_DIM
N_GROUPS_FOURIER = N_HEADS // 4
N_HEADS_SC = (N_HEADS - N_GROUPS_FOURIER) // 2
N_HEADS_CF = N_HEADS - N_GROUPS_FOURIER - N_HEADS_SC
D_SC = N_HEADS_SC * HEAD_DIM
D_FOURIER = N_GROUPS_FOURIER * HEAD_DIM
D_CF = N_HEADS_CF * HEAD_DIM
D_IN = 3 * D_SC + D_FOURIER + 2 * D_CF
SC_KERNEL = 3
CF_KERNEL = 31
FFN_KERNEL = 3
D_FF = ((8 * D_MODEL // 3 + 255) // 256) * 256
EPS = 1e-6

kernel_name = "hybrid_shortconv_fourier_conformer_encoder"


def rmsnorm(x, g):
    xf = x.astype(jnp.float32)
    y = xf * lax.rsqrt(jnp.mean(xf * xf, axis=-1, keepdims=True) + EPS)
    return y.astype(x.dtype) * g


def layernorm(x, g, b):
    xf = x.astype(jnp.float32)
    mu = jnp.mean(xf, axis=-1, keepdims=True)
    xc = xf - mu
    y = xc * lax.rsqrt(jnp.mean(xc * xc, axis=-1, keepdims=True) + EPS)
    return y.astype(x.dtype) * g + b


def dwconv(x, w):
    c = w.shape[1]
    return lax.conv_general_dilated(
        x, w[:, None, :].astype(x.dtype), window_strides=(1,), padding="SAME",
        dimension_numbers=("NWC", "WIO", "NWC"), feature_group_count=c)


def fourier_mix(u):
    b, s, _ = u.shape
    uf = u.astype(jnp.float32).reshape(b, s, N_GROUPS_FOURIER, HEAD_DIM)
    y = jnp.fft.fft2(uf, axes=(1, 3), norm="ortho").real
    return y.reshape(b, s, D_FOURIER).astype(u.dtype)


def mixer(x, norm_g, w_in, sc_conv_w, cf_conv_w, cf_conv_b, cf_ln_g, cf_ln_b, w_out):
    h = rmsnorm(x, norm_g)
    p = h @ w_in
    o1 = D_SC
    o2 = 2 * D_SC
    o3 = 3 * D_SC
    o4 = o3 + D_FOURIER
    o5 = o4 + D_CF
    b_gate, c_gate, v = p[..., :o1], p[..., o1:o2], p[..., o2:o3]
    f = p[..., o3:o4]
    cu, cg = p[..., o4:o5], p[..., o5:]
    y_sc = b_gate * dwconv(c_gate * v, sc_conv_w)
    y_f = fourier_mix(f)
    g = cu * jax.nn.sigmoid(cg)
    g = dwconv(g, cf_conv_w) + cf_conv_b
    y_cf = jax.nn.silu(layernorm(g, cf_ln_g, cf_ln_b))
    y = jnp.concatenate([y_sc, y_f, y_cf], axis=-1)
    return y @ w_out


def conv_ffn(x, norm_g, w_up, ffn_conv_w, w_down):
    h = rmsnorm(x, norm_g)
    up = h @ w_up
    gate, val = up[..., :D_FF], up[..., D_FF:]
    return (jax.nn.silu(dwconv(gate, ffn_conv_w)) * val) @ w_down


def trunk(x, norm_mix_g, w_in, sc_conv_w, cf_conv_w, cf_conv_b, cf_ln_g, cf_ln_b, w_out,
          norm_ffn_g, w_up, ffn_conv_w, w_down, final_norm_g):
    for l in range(DEPTH):
        x = x + mixer(x, norm_mix_g[l], w_in[l], sc_conv_w[l], cf_conv_w[l], cf_conv_b[l],
                      cf_ln_g[l], cf_ln_b[l], w_out[l])
        x = x + conv_ffn(x, norm_ffn_g[l], w_up[l], ffn_conv_w[l], w_down[l])
    return rmsnorm(x, final_norm_g)


def setup_inputs(seed: int = 0) -> dict:
    key = jax.random.key(seed)
    ks = jax.random.split(key, 18)
    f32 = jnp.float32
    nrm = lambda k, shape, scale: jax.random.normal(k, shape, f32) * scale
    return {
        "x_prompt": nrm(ks[0], (BATCH, SEQ, D_MODEL), 1.0),
        "x_sample": nrm(ks[1], (DEC_BATCH, DEC_SEQ, D_MODEL), 1.0),
        "norm_mix_g": 1.0 + nrm(ks[2], (DEPTH, D_MODEL), 0.02),
        "w_in": nrm(ks[3], (DEPTH, D_MODEL, D_IN), D_MODEL ** -0.5),
        "sc_conv_w": nrm(ks[4], (DEPTH, SC_KERNEL, D_SC), SC_KERNEL ** -0.5),
        "cf_conv_w": nrm(ks[5], (DEPTH, CF_KERNEL, D_CF), CF_KERNEL ** -0.5),
        "cf_conv_b": nrm(ks[6], (DEPTH, D_CF), 0.02),
        "cf_ln_g": 1.0 + nrm(ks[7], (DEPTH, D_CF), 0.02),
        "cf_ln_b": nrm(ks[8], (DEPTH, D_CF), 0.02),
        "w_out": nrm(ks[9], (DEPTH, D_MIX, D_MODEL), D_MIX ** -0.5),
        "norm_ffn_g": 1.0 + nrm(ks[10], (DEPTH, D_MODEL), 0.02),
        "w_up": nrm(ks[11], (DEPTH, D_MODEL, 2 * D_FF), D_MODEL ** -0.5),
        "ffn_conv_w": nrm(ks[12], (DEPTH, FFN_KERNEL, D_FF), FFN_KERNEL ** -0.5),
        "w_down": nrm(ks[13], (DEPTH, D_FF, D_MODEL), D_FF ** -0.5),
        "final_norm_g": 1.0 + nrm(ks[14], (D_MODEL,), 0.02),
    }


def reference(x_prompt, x_sample, norm_mix_g, w_in, sc_conv_w, cf_conv_w, cf_conv_b, cf_ln_g,
              cf_ln_b, w_out, norm_ffn_g, w_up, ffn_conv_w, w_down, final_norm_g):
    y_prompt = trunk(x_prompt, norm_mix_g, w_in, sc_conv_w, cf_conv_w, cf_conv_b, cf_ln_g,
                     cf_ln_b, w_out, norm_ffn_g, w_up, ffn_conv_w, w_down, final_norm_g)
    y_sample = trunk(x_sample, norm_mix_g, w_in, sc_conv_w, cf_conv_w, cf_conv_b, cf_ln_g,
                     cf_ln_b, w_out, norm_ffn_g, w_up, ffn_conv_w, w_down, final_norm_g)
    return (y_prompt, y_sample)
```

```python
import numpy as np
import concourse.bass as bass
import concourse.mybir as mybir
from concourse.bass_utils import run_bass_kernel_spmd

F32 = mybir.dt.float32
BF16 = mybir.dt.bfloat16
F32R = mybir.dt.float32r
AF = mybir.ActivationFunctionType
ALU = mybir.AluOpType
EPS = 1e-6
TT = 512
G = 8
SG = 2


class Cfg:
    def __init__(self, D=4096, NTOK=4096, DEPTH=2):
        self.D = D
        self.KD = D // 128
        nh = D // 128
        self.NF = nh // 4
        self.NSC = (nh - self.NF) // 2
        self.NCF = nh - self.NF - self.NSC
        self.NIN = 3 * self.NSC + self.NF + 2 * self.NCF
        self.DFF = ((8 * D // 3 + 255) // 256) * 256
        self.KF = self.DFF // 128
        self.NTOK = NTOK
        self.SEG = NTOK // 2
        self.NT = NTOK // TT
        self.NS = NTOK // 128
        self.DEPTH = DEPTH
        self.MQ = D // 512
        self.groups = [(s, min(G, self.KF - s)) for s in range(0, self.KF, G)]
        o = 0
        self.pv = {}
        for l in range(DEPTH):
            for name, n in (("gmix", self.KD), ("gffn", self.KD), ("scw", 3 * self.NSC),
                            ("cfw", 31 * self.NCF), ("cfb", self.NCF), ("lng", self.NCF),
                            ("lnb", self.NCF), ("ffw", 3 * self.KF)):
                self.pv[(name, l)] = o
                o += n
        self.pv[("gfin", 0)] = o
        o += self.KD
        self.pv[("B", 0)] = o
        o += 1
        self.NPV = o
        self.NCORES = 6
        self.UE = max(self.KD * 128, G * 512)
        self.units = []
        for l in range(DEPTH):
            for u in range(self.NIN):
                self.units.append(("in", l, u))
            for m in range(self.KD):
                self.units.append(("out", l, m))
            for j in range(self.KF):
                for h in range(2):
                    self.units.append(("up", l, j, h))
            for gi in range(len(self.groups)):
                for mq in range(self.MQ):
                    self.units.append(("dn", l, gi, mq))
        self.uid = {k: i for i, k in enumerate(self.units)}
        self.NU = len(self.units)

    def win_units(self):
        c = self
        u = []
        for j in range(c.NSC):
            u += [("c", j, c.NSC + j), ("v", j, 2 * c.NSC + j), ("b", j, j)]
        for j in range(c.NF):
            u.append(("f", j, 3 * c.NSC + j))
        for j in range(c.NCF):
            u += [("cg", j, 3 * c.NSC + c.NF + c.NCF + j), ("cu", j, 3 * c.NSC + c.NF + j)]
        return u


class Sem:
    def __init__(self, h, name):
        self.h = h
        self.name = name
        self.count = 0


class Buf:
    def __init__(self, name, sem=None):
        self.name = name
        self.sem = sem
        self.last_w = None
        self.readers = []


class Eng:
    def __init__(self, name, sem):
        self.name = name
        self.sem = sem
        self.ops = []
        self.waited = {}


class Prog:
    def __init__(self, nc, stack):
        self.nc = nc
        self.stack = stack
        self.sems = []
        self.eng = {}
        for n in ("tensor", "vector", "scalar", "gpsimd", "sync"):
            self.eng[n] = Eng(n, self.new_sem("done_" + n))
        self.nbuf = 0
        self.free_sems = []
        self.phase_sems = None

    def new_sem(self, name):
        h = self.stack.enter_context(self.nc.semaphore(name))
        s = Sem(h, name)
        self.sems.append(s)
        return s

    def buf(self, name, dma=False):
        self.nbuf += 1
        if not dma:
            return Buf(name)
        if self.phase_sems is not None and self.free_sems:
            s = self.free_sems.pop()
        else:
            s = self.new_sem("b%d" % self.nbuf)
        if self.phase_sems is not None:
            self.phase_sems.append(s)
        return Buf(name, s)

    def phase_begin(self):
        self.phase_sems = []

    def phase_end(self):
        self.barrier()
        self.free_sems.extend(self.phase_sems)
        self.phase_sems = None

    def op(self, eng, fn, reads=(), writes=(), dma=None):
        e = self.eng[eng]
        need = {}

        def add(ev):
            if ev is None:
                return
            s, v = ev
            if need.get(s.name, (None, 0))[1] < v:
                need[s.name] = (s, v)

        for b in reads:
            add(b.last_w)
        for b in writes:
            add(b.last_w)
            for r in b.readers:
                add(r)
        waits = []
        for s, v in need.values():
            if s is e.sem and eng == "tensor" and dma is None:
                continue
            if e.waited.get(s.name, 0) >= v:
                continue
            e.waited[s.name] = v
            waits.append((s, v))
        if dma is not None:
            s = dma.sem
            s.count += 16
            inc = (s, 16)
        else:
            s = e.sem
            s.count += 1
            inc = (s, 1)
        ev = (s, s.count)
        e.ops.append((waits, fn, inc))
        for b in writes:
            b.last_w = ev
            b.readers = []
        for b in reads:
            if b not in writes:
                b.readers.append(ev)
        return ev

    def barrier(self):
        for e in self.eng.values():
            waits = []
            for s in self.sems:
                if s.count > e.waited.get(s.name, 0):
                    e.waited[s.name] = s.count
                    waits.append((s, s.count))
            if waits:
                e.ops.append((waits, None, None))

    def emit(self, block):
        def make(e):
            def body(be):
                for waits, fn, inc in e.ops:
                    for s, v in waits:
                        be.wait_ge(s.h, v)
                    if fn is not None:
                        ins = fn(be)
                        ins.then_inc(inc[0].h, inc[1])
            return body

        block.tensor(make(self.eng["tensor"]))
        block.vector(make(self.eng["vector"]))
        block.scalar(make(self.eng["scalar"]))
        block.gpsimd(make(self.eng["gpsimd"]))
        block.sync(make(self.eng["sync"]))


class Arena:
    def __init__(self, nc, base, limit):
        self.nc = nc
        self.base = base
        self.off = base
        self.limit = limit
        self.n = 0

    def reset(self):
        self.off = self.base

    def alloc(self, name, shape, dtype):
        size = int(np.prod(shape[1:])) * (4 if dtype == F32 else 2)
        size = (size + 63) // 64 * 64
        assert self.off + size <= self.limit, (name, self.off, size, self.limit)
        self.n += 1
        t = self.nc.alloc_sbuf_tensor_at("%s_%d" % (name, self.n), list(shape), dtype, offset=self.off)
        self.last_off = self.off
        self.off += size
        return t

    def alloc_at(self, name, shape, dtype, off):
        self.n += 1
        return self.nc.alloc_sbuf_tensor_at("%s_%d" % (name, self.n), list(shape), dtype, offset=off)


def build_program(cfg):
    from contextlib import ExitStack

    c = cfg
    D, KD, NTOK, NT, NS, KF, DEPTH = c.D, c.KD, c.NTOK, c.NT, c.NS, c.KF, c.DEPTH
    NSC, NF, NCF, NIN, MQ, UE = c.NSC, c.NF, c.NCF, c.NIN, c.MQ, c.UE
    nc = bass.Bass("TRN2", target_bir_lowering=False)

    def din(name, shape):
        return nc.dram_tensor(name, list(shape), F32, kind="ExternalInput").ap()

    x_in = din("x", (NTOK, D))
    wsh_in = din("wsh", (c.NU, 128, UE))
    pvec_in = din("pvec", (128, c.NPV))
    rowtab_in = din("rowtab", (2, 2, NS, NTOK))
    ptab_in = din("ptab", (128, 2, NTOK))
    cdft_in = din("cdft", (128, 256))
    ident_in = din("ident", (128, 128))
    y_out = nc.dram_tensor("y", [NTOK, D], F32, kind="ExternalOutput").ap()

    def dscr(name, shape, dt):
        return nc.dram_tensor(name, list(shape), dt).ap()

    WCH = 192
    walls = [dscr("wall%d" % i, (min(WCH, c.NU - i * WCH), 128, UE), BF16) for i in range((c.NU + WCH - 1) // WCH)]
    uid = c.uid

    def wunit(u):
        return walls[u // WCH][u % WCH]

    conv = {"issued": 0, "done": 0}

    def wsrc(key, n=KD * 128):
        assert uid[key] < conv["done"], ("weight unit used before its bf16 conversion completed", key)
        return wunit(uid[key])[:, 0:n]

    dftb = dscr("dftb", (2, NT, 128, NS * 512), BF16)
    xT = dscr("xT", (KD, 128, NTOK), F32)
    h2T = dscr("h2T", (KD, 128, NTOK), BF16)
    yT = dscr("yT", (KD, 128, NTOK), BF16)
    cvT = dscr("cvT", (NSC, 128, NTOK), F32)
    bT = dscr("bT", (NSC, 128, NTOK), F32)
    ggT = dscr("ggT", (NCF, 128, NTOK), BF16)
    ghT = dscr("ghT", (NF, 128, NS, 256), BF16)

    stack = ExitStack()
    P = Prog(nc, stack)
    op = P.op

    SB_BASE = 16640
    SB_LIMIT = 229376 - 64
    pers = Arena(nc, SB_BASE, SB_LIMIT)
    ones_f = pers.alloc("ones", (128, 128), F32)
    ones_r = pers.alloc("onesr", (128, 128), F32)
    ident = pers.alloc("ident", (128, 128), F32)
    pvec = pers.alloc("pvec", (128, c.NPV), F32)
    cd_f = pers.alloc("cdf", (128, 256), F32)
    cd_b = pers.alloc("cdb", (128, 256), BF16)
    identb = pers.alloc("identb", (128, 128), BF16)
    hh = pers.alloc("hh", (128, KD, 2 * NT), BF16)
    ghalo = pers.alloc("ghalo", (128, KF, 2 * NT), F32)
    ghl = pers.alloc("ghl", (128, KF, NT), F32)
    ghr = pers.alloc("ghr", (128, KF, NT), F32)
    ar = Arena(nc, pers.off, SB_LIMIT)

    b_ones, b_ident, b_cdb = P.buf("ones"), P.buf("ident", True), P.buf("cdb")
    b_pvec, b_cdf = P.buf("pvec", True), P.buf("cdf", True)
    b_hh, b_ghalo, b_ghl, b_ghr = P.buf("hh"), P.buf("ghalo"), P.buf("ghl"), P.buf("ghr")

    psum = [stack.enter_context(nc.psum_tensor("ps%d" % i, [128, 512], F32)) for i in range(8)]
    b_ps = [P.buf("ps%d" % i) for i in range(8)]

    def pcol(name, l, j):
        o = c.pv[(name, l)] + j
        return pvec[:, o:o + 1]

    Bcol = pcol("B", 0, 0)

    op("gpsimd", lambda e: e.dma_start(out=pvec[:], in_=pvec_in), writes=[b_pvec], dma=b_pvec)
    op("gpsimd", lambda e: e.dma_start(out=ident[:], in_=ident_in), writes=[b_ident], dma=b_ident)
    op("gpsimd", lambda e: e.dma_start(out=cd_f[:], in_=cdft_in), writes=[b_cdf], dma=b_cdf)
    op("vector", lambda e: e.memset(ones_f[:], 1.0), writes=[b_ones])
    b_onesr = P.buf("onesr")
    op("vector", lambda e: e.tensor_copy(out=ones_r[:].bitcast(F32R), in_=ones_f[:]), reads=[b_ones], writes=[b_onesr])
    op("vector", lambda e: e.tensor_copy(out=cd_b[:], in_=cd_f[:]), reads=[b_cdf], writes=[b_cdb])
    b_identb = P.buf("identb")
    op("vector", lambda e: e.tensor_copy(out=identb[:], in_=ident[:]), reads=[b_ident], writes=[b_identb])
    rr = {"n": 0}

    def evac_eng():
        rr["n"] += 1
        return "vector" if rr["n"] % 2 else "scalar"

    def copy_op(eng, out, in_):
        if eng == "scalar":
            return lambda e: e.activation(out=out, in_=in_, func=AF.Copy)
        return lambda e: e.tensor_copy(out=out, in_=in_)

    b_bg = P.buf("bgconv", True)

    def bg(n):
        for _ in range(n):
            u = conv["issued"]
            if u >= c.NU:
                return
            conv["issued"] += 1
            op("gpsimd", lambda e, u=u: e.dma_start(out=wunit(u), in_=wsh_in[u]), dma=b_bg)

    def prepass_w0():
        bg(NIN)

    def spread(total, nslices, i):
        return ((i + 1) * total) // nslices - (i * total) // nslices

    def prepass_dft():
        ar.reset()
        ptab = ar.alloc("ptab", (128, 2, NTOK), F32)
        b_ptab = P.buf("ptab", True)
        bc = [[ar.alloc("bc", (128, 8, 512), F32) for _ in range(2)] for _ in range(2)]
        b_bc = [[P.buf("bc", True) for _ in range(2)] for _ in range(2)]
        m = [ar.alloc("m", (128, 8, 512), F32) for _ in range(2)]
        b_m = [P.buf("m") for _ in range(2)]
        ob = [ar.alloc("ob", (128, 8, 512), BF16) for _ in range(2)]
        b_ob = [P.buf("ob", True) for _ in range(2)]
        op("sync", lambda e: e.dma_start(out=ptab[:], in_=ptab_in), writes=[b_ptab], dma=b_ptab)
        n = 0
        for t in range(2):
            for kt in range(NT):
                for sq in range(NS // 8):
                    s = n % 2
                    n += 1
                    for q in range(2):
                        op("sync", lambda e, s=s, q=q, t=t, kt=kt, sq=sq: e.dma_start(
                            out=bc[s][q][:],
                            in_=rowtab_in[t, q, sq * 8:(sq + 1) * 8, kt * 512:(kt + 1) * 512].partition_broadcast(128)),
                           writes=[b_bc[s][q]], dma=b_bc[s][q])
                        op("vector", lambda e, s=s, q=q, kt=kt: e.tensor_tensor(
                            out=m[q][:], in0=bc[s][q][:],
                            in1=ptab[:, q, kt * 512:(kt + 1) * 512][:, None, :].to_broadcast([128, 8, 512]),
                            op=ALU.mult), reads=[b_bc[s][q], b_ptab], writes=[b_m[q]])
                    op("vector", lambda e, s=s: e.tensor_tensor(out=ob[s][:], in0=m[0][:], in1=m[1][:], op=ALU.add),
                       reads=[b_m[0], b_m[1]], writes=[b_ob[s]])
                    op("gpsimd", lambda e, s=s, t=t, kt=kt, sq=sq: e.dma_start(
                        out=dftb[t, kt, :, sq * 4096:(sq + 1) * 4096], in_=ob[s][:].rearrange("p a k -> p (a k)")),
                       reads=[b_ob[s]], dma=b_ob[s])

    def phase_T():
        ar.reset()
        xin = [ar.alloc("xin", (128, D), F32) for _ in range(2)]
        xst = [ar.alloc("xst", (128, KD, 128), F32) for _ in range(2)]
        b_xin = [P.buf("xin", True) for _ in range(2)]
        b_xst = [P.buf("xst", True) for _ in range(2)]
        pb = 0
        for tb in range(NS):
            s = tb % 2
            op("sync", lambda e, s=s, tb=tb: e.dma_start(out=xin[s][:], in_=x_in[tb * 128:(tb + 1) * 128, :]),
               writes=[b_xin[s]], dma=b_xin[s])
            bg(spread(KD, NS, tb))
            for c4 in range(KD // 4):
                bank = pb % 4
                pb += 1

                def tr(e, s=s, c4=c4, bank=bank):
                    for q in range(4):
                        cc = c4 * 4 + q
                        ins = e.transpose(out=psum[bank][:, q * 128:(q + 1) * 128],
                                          in_=xin[s][:, cc * 128:(cc + 1) * 128], identity=ident[:])
                    return ins
                op("tensor", tr, reads=[b_xin[s], b_ident], writes=[b_ps[bank]])
                eng = evac_eng()
                op(eng, copy_op(eng, xst[s][:, c4 * 4:(c4 + 1) * 4, :],
                                psum[bank][:].rearrange("p (q t) -> p q t", q=4)),
                   reads=[b_ps[bank]], writes=[b_xst[s]])
            op("gpsimd", lambda e, s=s, tb=tb: e.dma_start(
                out=xT[:, :, tb * 128:(tb + 1) * 128].rearrange("c p t -> p c t"), in_=xst[s][:]),
               reads=[b_xst[s]], dma=b_xst[s])

    class WRing:
        def __init__(self, name, n, elems):
            self.t = [ar.alloc(name, (128, elems), BF16) for _ in range(n)]
            self.b = [P.buf(name, True) for _ in range(n)]
            self.i = 0
            self.n = n

        def load(self, src, elems=None):
            s = self.i % self.n
            self.i += 1
            t = self.t[s]
            if elems is None:
                op("sync", lambda e: e.dma_start(out=t[:], in_=src), writes=[self.b[s]], dma=self.b[s])
            else:
                op("sync", lambda e: e.dma_start(out=t[:, 0:elems], in_=src), writes=[self.b[s]], dma=self.b[s])
            return t, self.b[s]

    def norm_stats(xs, b_xs, sq, b_sq, rstd, b_rstd, nbank, n=TT):
        for cc in range(KD):
            s = cc % 2
            op("scalar", lambda e, cc=cc, s=s: e.activation(out=sq[s][:, 0:n].bitcast(F32R), in_=xs[:, cc, :],
                                                          func=AF.Square),
               reads=[b_xs], writes=[b_sq[s]])
            op("tensor", lambda e, cc=cc, s=s: e.matmul(psum[nbank][:, 0:n], lhsT=ones_r[:].bitcast(F32R),
                                                      rhs=sq[s][:, 0:n].bitcast(F32R),
                                                      start=(cc == 0), stop=(cc == KD - 1)),
               reads=[b_sq[s], b_ones], writes=[b_ps[nbank]])
        op("vector", lambda e: e.tensor_scalar(out=rstd[:, 0:n], in0=psum[nbank][:, 0:n], scalar1=1.0 / D,
                                               scalar2=EPS, op0=ALU.mult, op1=ALU.add),
           reads=[b_ps[nbank]], writes=[b_rstd])
        op("scalar", lambda e: e.activation(out=rstd[:, 0:n], in_=rstd[:, 0:n], func=AF.Sqrt),
           reads=[b_rstd], writes=[b_rstd])
        op("vector", lambda e: e.reciprocal(out=rstd[:, 0:n], in_=rstd[:, 0:n]), reads=[b_rstd], writes=[b_rstd])

    def norm_apply(xs, b_xs, gname, l, out, b_out, rstd, b_rstd, n=TT):
        for cc in range(KD):
            op("vector", lambda e, cc=cc: e.scalar_tensor_tensor(
                out=out[:, cc, :], in0=xs[:, cc, :], scalar=pcol(gname, l, cc), in1=rstd[:, 0:n],
                op0=ALU.mult, op1=ALU.mult),
               reads=[b_xs, b_rstd, b_pvec], writes=[b_out])

    def rmsnorm_tile(xs, b_xs, gname, l, out, b_out, sq, b_sq, rstd, b_rstd, nbank, n=TT):
        norm_stats(xs, b_xs, sq, b_sq, rstd, b_rstd, nbank, n)
        norm_apply(xs, b_xs, gname, l, out, b_out, rstd, b_rstd, n)

    def mm_group(bank, wt, b_w, act, b_act, nk, n=512, woff=0):
        def f(e):
            for k in range(nk):
                ins = e.matmul(psum[bank][:, 0:n], lhsT=wt[:, woff + k * 128: woff + (k + 1) * 128],
                               rhs=act[:, k, 0:n], start=(k == 0), stop=(k == nk - 1))
            return ins
        op("tensor", f, reads=[b_w, b_act], writes=[b_ps[bank]])

    def phase_A(l):
        ar.reset()
        xs = ar.alloc("xs", (128, KD, TT), F32)
        h = ar.alloc("h", (128, KD, TT), BF16)
        sq = [ar.alloc("sq", (128, TT), F32) for _ in range(2)]
        rstd2 = [ar.alloc("rstd", (128, TT), F32) for _ in range(2)]
        b_rstd2 = [P.buf("rstd") for _ in range(2)]
        tmp = [ar.alloc("tmp", (128, TT), F32) for _ in range(2)]
        stg = {k: [ar.alloc("stg" + k, (128, SG, TT), BF16 if k == "gg" else F32) for _ in range(2)]
               for k in ("cv", "b", "gg")}
        fb = [ar.alloc("fb", (128, TT), BF16) for _ in range(2)]
        ghst = ar.alloc("ghst", (128, NF, 4, 256), BF16)
        wr = WRing("wA", 3, KD * 128)
        b_xs, b_h = P.buf("xs", True), P.buf("h")
        b_sq = [P.buf("sq") for _ in range(2)]
        b_tmp = [P.buf("tmp") for _ in range(2)]
        b_stg = {k: [P.buf("stg" + k, True) for _ in range(2)] for k in ("cv", "b", "gg")}

        def load_x(i):
            t0 = i * TT
            op("sync", lambda e: e.dma_start(out=xs[:], in_=xT[:, :, t0:t0 + TT].rearrange("c p t -> p c t")),
               writes=[b_xs], dma=b_xs)

        load_x(0)
        norm_stats(xs, b_xs, sq, b_sq, rstd2[0], b_rstd2[0], 0)
        b_fb = [P.buf("fb") for _ in range(2)]
        b_ghst = P.buf("ghst", True)
        units = c.win_units()
        dstT = {"cv": cvT, "b": bT, "gg": ggT}
        nchunks = {"cv": NSC, "b": NSC, "gg": NCF}
        for i in range(NT):
            t0 = i * TT
            norm_apply(xs, b_xs, "gmix", l, h, b_h, rstd2[i % 2], b_rstd2[i % 2])
            if i + 1 < NT:
                load_x(i + 1)
            pbank = 0
            tcount = 0
            ghbank = 0

            def stage_slot(kind, j):
                return stg[kind][(j // SG) % 2], b_stg[kind][(j // SG) % 2]

            def maybe_store(kind, j, t0=t0):
                n = nchunks[kind]
                if j % SG == SG - 1 or j == n - 1:
                    j0 = (j // SG) * SG
                    cnt = j - j0 + 1
                    st, bst = stage_slot(kind, j)
                    op("gpsimd", lambda e: e.dma_start(
                        out=dstT[kind][j0:j0 + cnt, :, t0:t0 + TT].rearrange("c p t -> p c t"),
                        in_=st[:, 0:cnt, :]), reads=[bst], dma=bst)

            for u, (kind, j, _) in enumerate(units):
                if l == 0:
                    bg(spread(spread(2 * KF, NT, i), NIN, u))
                if u == NIN // 3 and i + 1 < NT:
                    norm_stats(xs, b_xs, sq, b_sq, rstd2[(i + 1) % 2], b_rstd2[(i + 1) % 2], 0)
                wt, b_w = wr.load(wsrc(("in", l, u)))
                bank = 1 + pbank % 4
                pbank += 1
                mm_group(bank, wt, b_w, h, b_h, KD)
                ps = psum[bank]
                bps = b_ps[bank]
                if kind == "c":
                    ts = tcount % 2
                    tcount += 1
                    op("scalar", copy_op("scalar", tmp[ts][:], ps[:]), reads=[bps], writes=[b_tmp[ts]])
                    last_tmp = ts
                elif kind == "v":
                    st, bst = stage_slot("cv", j)
                    op("vector", lambda e, st=st, j=j, ps=ps, ts=last_tmp: e.tensor_tensor(
                        out=st[:, j % SG, :], in0=tmp[ts][:], in1=ps[:], op=ALU.mult),
                       reads=[bps, b_tmp[last_tmp]], writes=[bst])
                    maybe_store("cv", j)
                elif kind == "b":
                    st, bst = stage_slot("b", j)
                    op("scalar", copy_op("scalar", st[:, j % SG, :], ps[:]), reads=[bps], writes=[bst])
                    maybe_store("b", j)
                elif kind == "f":
                    s = j % 2
                    op("scalar", copy_op("scalar", fb[s][:], ps[:]), reads=[bps], writes=[b_fb[s]])
                    for half in range(2):
                        gb = 5 + ghbank % 2
                        ghbank += 1

                        def chmm(e, s=s, half=half, gb=gb):
                            for q in range(2):
                                ts_ = half * 2 + q
                                ins = e.matmul(psum[gb][:, q * 256:(q + 1) * 256],
                                               lhsT=fb[s][:, ts_ * 128:(ts_ + 1) * 128], rhs=cd_b[:],
                                               start=True, stop=True)
                            return ins
                        op("tensor", chmm, reads=[b_fb[s], b_cdb], writes=[b_ps[gb]])
                        op("vector", lambda e, j=j, half=half, gb=gb: e.tensor_copy(
                            out=ghst[:, j, half * 2:half * 2 + 2, :],
                            in_=psum[gb][:].rearrange("p (q c) -> p q c", q=2)),
                           reads=[b_ps[gb]], writes=[b_ghst])
                    if j == NF - 1:
                        op("gpsimd", lambda e, i=i: e.dma_start(
                            out=ghT[:, :, i * 4:(i + 1) * 4, :].rearrange("j p s c -> p j s c"), in_=ghst[:]),
                           reads=[b_ghst], dma=b_ghst)
                elif kind == "cg":
                    ts = tcount % 2
                    tcount += 1
                    op("scalar", lambda e, ts=ts, ps=ps: e.activation(out=tmp[ts][:], in_=ps[:], func=AF.Sigmoid),
                       reads=[bps], writes=[b_tmp[ts]])
                    last_tmp = ts
                elif kind == "cu":
                    st, bst = stage_slot("gg", j)
                    op("vector", lambda e, st=st, j=j, ps=ps, ts=last_tmp: e.tensor_tensor(
                        out=st[:, j % SG, :], in0=tmp[ts][:], in1=ps[:], op=ALU.mult),
                       reads=[bps, b_tmp[last_tmp]], writes=[bst])
                    maybe_store("gg", j)

    def phase_A2(l):
        ar.reset()
        cvws = [ar.alloc("cvw", (128, NSC, TT + 2), F32) for _ in range(2)]
        bw = ar.alloc("bw", (128, NSC, TT), F32)
        ggws = [ar.alloc("ggw", (128, NCF, TT + 30), BF16) for _ in range(2)]
        dg = [ar.alloc("dg", (128, 31, 128), BF16) for _ in range(2)]
        b_dg = [P.buf("dg") for _ in range(2)]
        co = ar.alloc("co", (128, NCF, TT), F32)
        acc = [ar.alloc("acc", (128, TT), F32) for _ in range(2)]
        sq = [ar.alloc("sq2", (128, TT), F32) for _ in range(2)]
        mean = ar.alloc("mean", (128, TT), F32)
        var = ar.alloc("var", (128, TT), F32)
        ysc = ar.alloc("ysc", (128, NSC, TT), BF16)
        ycf = ar.alloc("ycf", (128, NCF, TT), BF16)
        b_cvws = [P.buf("cvw", True) for _ in range(2)]
        b_ggws = [P.buf("ggw", True) for _ in range(2)]
        b_bw = P.buf("bw", True)
        b_co, b_mean, b_var = P.buf("co"), P.buf("mean"), P.buf("var")
        b_acc = [P.buf("acc") for _ in range(2)]
        b_sq = [P.buf("sq2") for _ in range(2)]
        b_ysc, b_ycf = P.buf("ysc", True), P.buf("ycf", True)
        DCF = NCF * 128

        def load_window(dst, b_dst, src, halo, i):
            t0 = i * TT
            lo = max(t0 - halo, 0)
            hi = min(t0 + TT + halo, NTOK)
            dlo = lo - (t0 - halo)
            if dlo > 0:
                op("vector", lambda e: e.memset(dst[:, :, 0:dlo], 0.0), writes=[b_dst])
            if hi < t0 + TT + halo:
                op("vector", lambda e: e.memset(dst[:, :, TT + 2 * halo - (t0 + TT + halo - hi):TT + 2 * halo], 0.0),
                   writes=[b_dst])
            op("sync", lambda e: e.dma_start(out=dst[:, :, dlo:dlo + (hi - lo)],
                                             in_=src[:, :, lo:hi].rearrange("c p t -> p c t")),
               writes=[b_dst], dma=b_dst)
            if i == NT // 2 - 1:
                op("vector", lambda e: e.tensor_scalar(out=dst[:, :, TT + halo:TT + 2 * halo],
                                                       in0=dst[:, :, TT + halo:TT + 2 * halo],
                                                       scalar1=Bcol, scalar2=None, op0=ALU.mult),
                   reads=[b_pvec], writes=[b_dst])
            if i == NT // 2:
                op("vector", lambda e: e.tensor_scalar(out=dst[:, :, 0:halo], in0=dst[:, :, 0:halo],
                                                       scalar1=Bcol, scalar2=None, op0=ALU.mult),
                   reads=[b_pvec], writes=[b_dst])

        def loads(i):
            load_window(cvws[i % 2], b_cvws[i % 2], cvT, 1, i)
            load_window(ggws[i % 2], b_ggws[i % 2], ggT, 15, i)

        def load_b(i):
            t0 = i * TT
            op("sync", lambda e: e.dma_start(out=bw[:], in_=bT[:, :, t0:t0 + TT].rearrange("c p t -> p c t")),
               writes=[b_bw], dma=b_bw)

        def do_tile(i, cvw, b_cvw, ggw, b_ggw):
            t0 = i * TT
            if l == 0:
                bg(spread(len(c.groups) * MQ, NT, i))
            if i + 1 < NT:
                loads(i + 1)
            for j in range(NSC):
                a = j % 2
                w = [pcol("scw", l, k * NSC + j) for k in range(3)]
                op("scalar", lambda e, a=a, j=j, w=w: e.activation(out=acc[a][:], in_=cvw[:, j, 0:TT], func=AF.Identity,
                                                                 scale=w[0]),
                   reads=[b_cvw, b_pvec], writes=[b_acc[a]])
                for k in (1, 2):
                    op("vector", lambda e, a=a, j=j, w=w, k=k: e.scalar_tensor_tensor(
                        out=acc[a][:], in0=cvw[:, j, k:k + TT], scalar=w[k], in1=acc[a][:],
                        op0=ALU.mult, op1=ALU.add), reads=[b_cvw, b_pvec, b_acc[a]], writes=[b_acc[a]])
                op("vector", lambda e, a=a, j=j: e.tensor_tensor(out=ysc[:, j, :], in0=acc[a][:], in1=bw[:, j, :],
                                                               op=ALU.mult),
                   reads=[b_acc[a], b_bw], writes=[b_ysc])
            op("gpsimd", lambda e, t0=t0: e.dma_start(out=yT[0:NSC, :, t0:t0 + TT].rearrange("c p t -> p c t"),
                                                      in_=ysc[:]), reads=[b_ysc], dma=b_ysc)
            if i + 1 < NT:
                load_b(i + 1)
            for j in range(NCF):
                d = j % 2
                o = c.pv[("cfw", l)] + j
                wk = pvec[:, o:o + 31 * NCF:NCF]
                op("vector", lambda e, d=d, wk=wk: e.tensor_tensor(
                    out=dg[d][:], in0=identb[:, None, :].to_broadcast([128, 31, 128]),
                    in1=wk[:, :, None].to_broadcast([128, 31, 128]), op=ALU.mult),
                   reads=[b_identb, b_pvec], writes=[b_dg[d]])
                cb = 2 + j % 4

                def cmm(e, d=d, j=j, cb=cb):
                    for k in range(31):
                        ins = e.matmul(psum[cb][:], lhsT=dg[d][:, k, :], rhs=ggw[:, j, k:k + TT],
                                       start=(k == 0), stop=(k == 30))
                    return ins
                op("tensor", cmm, reads=[b_dg[d], b_ggw], writes=[b_ps[cb]])
                if j % 2 == 0:
                    op("scalar", lambda e, j=j, cb=cb: e.activation(out=co[:, j, :], in_=psum[cb][:], func=AF.Identity,
                                                                  bias=pcol("cfb", l, j)),
                       reads=[b_ps[cb], b_pvec], writes=[b_co])
                else:
                    op("vector", lambda e, j=j, cb=cb: e.tensor_scalar(out=co[:, j, :], in0=psum[cb][:],
                                                                     scalar1=pcol("cfb", l, j), scalar2=None,
                                                                     op0=ALU.add),
                       reads=[b_ps[cb], b_pvec], writes=[b_co])
                s = j % 2
                op("scalar", lambda e, j=j, s=s: e.activation(out=sq[s][:].bitcast(F32R), in_=co[:, j, :],
                                                            func=AF.Square),
                   reads=[b_co], writes=[b_sq[s]])
                op("tensor", lambda e, j=j: e.matmul(psum[0][:], lhsT=ones_f[:], rhs=co[:, j, :],
                                                     start=(j == 0), stop=(j == NCF - 1)),
                   reads=[b_co, b_ones], writes=[b_ps[0]])
                op("tensor", lambda e, j=j, s=s: e.matmul(psum[1][:], lhsT=ones_r[:].bitcast(F32R),
                                                        rhs=sq[s][:].bitcast(F32R),
                                                        start=(j == 0), stop=(j == NCF - 1)),
                   reads=[b_sq[s], b_ones], writes=[b_ps[1]])
            op("scalar", lambda e: e.activation(out=mean[:], in_=psum[0][:], func=AF.Identity, scale=1.0 / DCF),
               reads=[b_ps[0]], writes=[b_mean])
            op("vector", lambda e: e.tensor_tensor(out=var[:], in0=mean[:], in1=mean[:], op=ALU.mult),
               reads=[b_mean], writes=[b_var])
            op("vector", lambda e: e.scalar_tensor_tensor(out=var[:], in0=psum[1][:], scalar=1.0 / DCF, in1=var[:],
                                                          op0=ALU.mult, op1=ALU.subtract),
               reads=[b_ps[1], b_var], writes=[b_var])
            op("vector", lambda e: e.tensor_scalar(out=var[:], in0=var[:], scalar1=EPS, scalar2=None,
                                                   op0=ALU.add), reads=[b_var], writes=[b_var])
            op("scalar", lambda e: e.activation(out=var[:], in_=var[:], func=AF.Sqrt),
               reads=[b_var], writes=[b_var])
            op("vector", lambda e: e.reciprocal(out=var[:], in_=var[:]), reads=[b_var], writes=[b_var])
            for j in range(NCF):
                a = j % 2
                op("vector", lambda e, j=j, a=a: e.tensor_tensor(out=acc[a][:], in0=co[:, j, :], in1=mean[:],
                                                               op=ALU.subtract),
                   reads=[b_co, b_mean], writes=[b_acc[a]])
                op("vector", lambda e, a=a: e.tensor_tensor(out=acc[a][:], in0=acc[a][:], in1=var[:], op=ALU.mult),
                   reads=[b_acc[a], b_var], writes=[b_acc[a]])
                op("scalar", lambda e, j=j, a=a: e.activation(out=ycf[:, j, :], in_=acc[a][:], func=AF.Silu,
                                                            bias=pcol("lnb", l, j), scale=pcol("lng", l, j)),
                   reads=[b_acc[a], b_pvec], writes=[b_ycf])
            op("gpsimd", lambda e, t0=t0: e.dma_start(
                out=yT[NSC + NF:NSC + NF + NCF, :, t0:t0 + TT].rearrange("c p t -> p c t"), in_=ycf[:]),
               reads=[b_ycf], dma=b_ycf)

        loads(0)
        load_b(0)
        for i in range(NT):
            do_tile(i, cvws[i % 2], b_cvws[i % 2], ggws[i % 2], b_ggws[i % 2])


    def phase_B(l):
        ar.reset()
        tab = [[ar.alloc("tab", (128, NS, 512), BF16) for _ in range(2)] for _ in range(2)]
        b_tab = [[P.buf("tab", True) for _ in range(2)] for _ in range(2)]
        gh = [ar.alloc("gh", (128, NS, 256), BF16) for _ in range(2)]
        b_gh = [P.buf("gh", True) for _ in range(2)]
        yf = [ar.alloc("yf", (128, NF, 512), BF16) for _ in range(2)]
        b_yf = [P.buf("yf", True) for _ in range(2)]
        gcount = 0
        pb = 0
        for kt in range(NT):
            s = kt % 2
            for t in range(2):
                op("sync", lambda e, s=s, t=t, kt=kt: e.dma_start(
                    out=tab[s][t][:].rearrange("p s k -> p (s k)"), in_=dftb[t, kt]),
                   writes=[b_tab[s][t]], dma=b_tab[s][t])
            for g in range(NF):
                gs = gcount % 2
                gcount += 1
                op("gpsimd", lambda e, gs=gs, g=g: e.dma_start(out=gh[gs][:], in_=ghT[g]),
                   writes=[b_gh[gs]], dma=b_gh[gs])
                bank = 2 + pb % 4
                pb += 1

                def dmm(e, s=s, gs=gs, bank=bank):
                    n = 0
                    for t in range(2):
                        for sc in range(NS):
                            ins = e.matmul(psum[bank][:], lhsT=gh[gs][:, sc, t * 128:(t + 1) * 128],
                                           rhs=tab[s][t][:, sc, :], start=(n == 0), stop=(n == 2 * NS - 1))
                            n += 1
                    return ins
                op("tensor", dmm, reads=[b_gh[gs], b_tab[s][0], b_tab[s][1]], writes=[b_ps[bank]])
                eng = evac_eng()
                op(eng, copy_op(eng, yf[s][:, g, :], psum[bank][:]), reads=[b_ps[bank]], writes=[b_yf[s]])
            op("gpsimd", lambda e, s=s, kt=kt: e.dma_start(
                out=yT[NSC:NSC + NF, :, kt * 512:(kt + 1) * 512].rearrange("c p t -> p c t"), in_=yf[s][:]),
               reads=[b_yf[s]], dma=b_yf[s])

    def phase_C1(l):
        ar.reset()
        TH = TT // 2
        NST = 2 * NT
        xs = [ar.alloc("xs", (128, KD, TH), F32) for _ in range(2)]
        yt = [ar.alloc("yt", (128, KD, TH), BF16) for _ in range(2)]
        h2 = [ar.alloc("h2", (128, KD, TH), BF16) for _ in range(2)]
        sq = [ar.alloc("sq", (128, TH), F32) for _ in range(2)]
        rstd = ar.alloc("rstd", (128, TH), F32)
        wr = WRing("wC1", 4, KD * 128)
        b_xs = [P.buf("xs", True) for _ in range(2)]
        b_yt = [P.buf("yt", True) for _ in range(2)]
        b_h2 = [P.buf("h2", True) for _ in range(2)]
        b_rstd = P.buf("rstd")
        b_sq = [P.buf("sq") for _ in range(2)]
        pb = 0

        def loads(st):
            p = st % 2
            t0 = st * TH
            op("sync", lambda e: e.dma_start(out=xs[p][:], in_=xT[:, :, t0:t0 + TH].rearrange("c p t -> p c t")),
               writes=[b_xs[p]], dma=b_xs[p])
            op("sync", lambda e: e.dma_start(out=yt[p][:], in_=yT[:, :, t0:t0 + TH].rearrange("c p t -> p c t")),
               writes=[b_yt[p]], dma=b_yt[p])

        loads(0)
        for st in range(NST):
            p = st % 2
            t0 = st * TH
            i = st // 2
            for m in range(KD):
                if m == KD // 4 and st + 1 < NST:
                    loads(st + 1)
                wt, b_w = wr.load(wsrc(("out", l, m)))
                bank = 1 + pb % 4
                pb += 1
                mm_group(bank, wt, b_w, yt[p], b_yt[p], KD, n=TH)
                op("vector", lambda e, m=m, bank=bank, p=p: e.tensor_tensor(
                    out=xs[p][:, m, :], in0=xs[p][:, m, :], in1=psum[bank][:, 0:TH], op=ALU.add),
                   reads=[b_ps[bank], b_xs[p]], writes=[b_xs[p]])
            rmsnorm_tile(xs[p], b_xs[p], "gffn", l, h2[p], b_h2[p], sq, b_sq, rstd, b_rstd, 0, n=TH)
            if st % 2 == 0:
                op("vector", lambda e, i=i, p=p: e.tensor_copy(out=hh[:, :, 2 * i:2 * i + 1], in_=h2[p][:, :, 0:1]),
                   reads=[b_h2[p]], writes=[b_hh])
            else:
                op("vector", lambda e, i=i, p=p: e.tensor_copy(out=hh[:, :, 2 * i + 1:2 * i + 2],
                                                             in_=h2[p][:, :, TH - 1:TH]),
                   reads=[b_h2[p]], writes=[b_hh])
            op("gpsimd", lambda e, t0=t0, p=p: e.dma_start(out=xT[:, :, t0:t0 + TH].rearrange("c p t -> p c t"),
                                                         in_=xs[p][:]), reads=[b_xs[p]], dma=b_xs[p])
            op("gpsimd", lambda e, t0=t0, p=p: e.dma_start(out=h2T[:, :, t0:t0 + TH].rearrange("c p t -> p c t"),
                                                         in_=h2[p][:]), reads=[b_h2[p]], dma=b_h2[p])

    def phase_H(l):
        ar.reset()
        wr = WRing("wH", 3, KD * 128)
        NH = 2 * NT
        pb = 0
        for j in range(KF):
            wt, b_w = wr.load(wsrc(("up", l, j, 0)))
            bank = 1 + pb % 4
            pb += 1
            mm_group(bank, wt, b_w, hh, b_hh, KD, n=NH)
            eng = evac_eng()
            op(eng, copy_op(eng, ghalo[:, j, :], psum[bank][:, 0:NH]), reads=[b_ps[bank]], writes=[b_ghalo])
        gv = ghalo[:].rearrange("p j (i two) -> p j i two", two=2)
        op("vector", lambda e: e.memset(ghl[:, :, 0:1], 0.0), writes=[b_ghl])
        op("vector", lambda e: e.tensor_copy(out=ghl[:, :, 1:NT], in_=gv[:, :, 0:NT - 1, 1]),
           reads=[b_ghalo], writes=[b_ghl])
        op("vector", lambda e: e.memset(ghr[:, :, NT - 1:NT], 0.0), writes=[b_ghr])
        op("vector", lambda e: e.tensor_copy(out=ghr[:, :, 0:NT - 1], in_=gv[:, :, 1:NT, 0]),
           reads=[b_ghalo], writes=[b_ghr])
        mid = NT // 2
        op("vector", lambda e: e.tensor_scalar(out=ghl[:, :, mid:mid + 1], in0=ghl[:, :, mid:mid + 1],
                                               scalar1=Bcol, scalar2=None, op0=ALU.mult),
           reads=[b_pvec, b_ghl], writes=[b_ghl])
        op("vector", lambda e: e.tensor_scalar(out=ghr[:, :, mid - 1:mid], in0=ghr[:, :, mid - 1:mid],
                                               scalar1=Bcol, scalar2=None, op0=ALU.mult),
           reads=[b_pvec, b_ghr], writes=[b_ghr])

    def phase_C2(l):
        ar.reset()
        last = (l == DEPTH - 1)
        xs = ar.alloc("xs", (128, KD, TT), F32)
        h2 = ar.alloc("h2", (128, KD, TT), BF16)
        h2_off = ar.last_off
        aa = [ar.alloc("a", (128, G, TT), BF16) for _ in range(2)]
        acc = [ar.alloc("acc", (128, TT), F32) for _ in range(2)]
        sil = [ar.alloc("sil", (128, TT), F32) for _ in range(2)]
        wu = WRing("wU", 4, KD * 128)
        wd = WRing("wD", 3, G * 512)
        b_xs, b_h2 = P.buf("xs", True), P.buf("h2", True)
        b_aa = [P.buf("a") for _ in range(2)]
        b_acc = [P.buf("acc") for _ in range(2)]
        b_sil = [P.buf("sil") for _ in range(2)]
        if last:
            sq = [ar.alloc("sqf", (128, TT), F32) for _ in range(2)]
            b_sq = [P.buf("sqf") for _ in range(2)]
            rstd = sil[0]
            b_rstd = b_sil[0]
            ost = [ar.alloc_at("ost", (128, D), F32, h2_off + o_ * (D * 2)) for o_ in range(2)]
            assert 2 * D * 4 <= KD * TT * 2 + 2 * D * 2
            b_ost = [P.buf("ost", True) for _ in range(2)]
        cnt = {"gv": 0, "d": 0, "k": 0, "o": 0}

        def up_group(i, gi):
            j0, gn = c.groups[gi]
            a_s = gi % 2
            for jj in range(gn):
                j = j0 + jj
                wg, b_wg = wu.load(wsrc(("up", l, j, 0)))
                wv, b_wv = wu.load(wsrc(("up", l, j, 1)))
                s = cnt["gv"] % 2
                cnt["gv"] += 1
                bg, bv = 0 + s, 2 + s
                mm_group(bg, wg, b_wg, h2, b_h2, KD)
                mm_group(bv, wv, b_wv, h2, b_h2, KD)
                k = cnt["k"] % 2
                cnt["k"] += 1
                w = [pcol("ffw", l, q * KF + j) for q in range(3)]
                op("scalar", lambda e, k=k, bg=bg, w=w: e.activation(out=acc[k][:], in_=psum[bg][:], func=AF.Identity,
                                                                   scale=w[1]),
                   reads=[b_ps[bg], b_pvec], writes=[b_acc[k]])
                op("vector", lambda e, k=k, bg=bg, w=w: e.scalar_tensor_tensor(
                    out=acc[k][:, 1:TT], in0=psum[bg][:, 0:TT - 1], scalar=w[0], in1=acc[k][:, 1:TT],
                    op0=ALU.mult, op1=ALU.add), reads=[b_ps[bg], b_acc[k], b_pvec], writes=[b_acc[k]])
                op("vector", lambda e, k=k, bg=bg, w=w: e.scalar_tensor_tensor(
                    out=acc[k][:, 0:TT - 1], in0=psum[bg][:, 1:TT], scalar=w[2], in1=acc[k][:, 0:TT - 1],
                    op0=ALU.mult, op1=ALU.add), reads=[b_ps[bg], b_acc[k], b_pvec], writes=[b_acc[k]])
                op("vector", lambda e, k=k, j=j, w=w, i=i: e.scalar_tensor_tensor(
                    out=acc[k][:, 0:1], in0=ghl[:, j, i:i + 1], scalar=w[0], in1=acc[k][:, 0:1],
                    op0=ALU.mult, op1=ALU.add), reads=[b_ghl, b_acc[k], b_pvec], writes=[b_acc[k]])
                op("vector", lambda e, k=k, j=j, w=w, i=i: e.scalar_tensor_tensor(
                    out=acc[k][:, TT - 1:TT], in0=ghr[:, j, i:i + 1], scalar=w[2], in1=acc[k][:, TT - 1:TT],
                    op0=ALU.mult, op1=ALU.add), reads=[b_ghr, b_acc[k], b_pvec], writes=[b_acc[k]])
                op("scalar", lambda e, k=k: e.activation(out=sil[k][:], in_=acc[k][:], func=AF.Silu),
                   reads=[b_acc[k]], writes=[b_sil[k]])
                op("vector", lambda e, k=k, bv=bv, a_s=a_s, jj=jj: e.tensor_tensor(
                    out=aa[a_s][:, jj, :], in0=sil[k][:], in1=psum[bv][:], op=ALU.mult),
                   reads=[b_sil[k], b_ps[bv]], writes=[b_aa[a_s]])

        def down_group(i, gi):
            j0, gn = c.groups[gi]
            a_s = gi % 2
            for mq in range(MQ):
                wt, b_w = wd.load(wsrc(("dn", l, gi, mq), gn * 512), elems=gn * 512)
                for m4 in range(4):
                    m = mq * 4 + m4
                    bank = 4 + cnt["d"] % 3
                    cnt["d"] += 1

                    def f(e, wt=wt, m4=m4, bank=bank, a_s=a_s, gn=gn):
                        for jj in range(gn):
                            ins = e.matmul(psum[bank][:], lhsT=wt[:, jj * 512 + m4 * 128: jj * 512 + (m4 + 1) * 128],
                                           rhs=aa[a_s][:, jj, :], start=(jj == 0), stop=(jj == gn - 1))
                        return ins
                    op("tensor", f, reads=[b_w, b_aa[a_s]], writes=[b_ps[bank]])
                    op("vector", lambda e, m=m, bank=bank: e.tensor_tensor(out=xs[:, m, :], in0=xs[:, m, :],
                                                                         in1=psum[bank][:], op=ALU.add),
                       reads=[b_ps[bank], b_xs], writes=[b_xs])

        def load_xs(i, q):
            t0 = i * TT
            op(q, lambda e: e.dma_start(out=xs[:], in_=xT[:, :, t0:t0 + TT].rearrange("c p t -> p c t")),
               writes=[b_xs], dma=b_xs)

        def load_h2(i, q):
            t0 = i * TT
            op(q, lambda e: e.dma_start(out=h2[:], in_=h2T[:, :, t0:t0 + TT].rearrange("c p t -> p c t")),
               writes=[b_h2], dma=b_h2)

        ng = len(c.groups)
        for i in range(NT):
            t0 = i * TT
            if last or i == 0:
                load_xs(i, "gpsimd")
                load_h2(i, "sync" if i == 0 else "gpsimd")
            for gi in range(ng):
                if not last:
                    bg(spread(spread(c.NU // DEPTH, NT, i), ng, gi))
                up_group(i, gi)
                if gi >= 1:
                    down_group(i, gi - 1)
            if not last and i + 1 < NT:
                load_h2(i + 1, "sync")
            down_group(i, ng - 1)
            if not last:
                op("gpsimd", lambda e, t0=t0: e.dma_start(out=xT[:, :, t0:t0 + TT].rearrange("c p t -> p c t"),
                                                          in_=xs[:]), reads=[b_xs], dma=b_xs)
                if i + 1 < NT:
                    load_xs(i + 1, "gpsimd")
            else:
                rmsnorm_tile(xs, b_xs, "gfin", 0, xs, b_xs, sq, b_sq, rstd, b_rstd, 7)
                for ts in range(4):
                    o = cnt["o"] % 2
                    cnt["o"] += 1
                    for c4 in range(KD // 4):
                        bank = 4 + cnt["d"] % 3
                        cnt["d"] += 1

                        def tr(e, ts=ts, c4=c4, bank=bank):
                            for q in range(4):
                                cc = c4 * 4 + q
                                ins = e.transpose(out=psum[bank][:, q * 128:(q + 1) * 128],
                                                  in_=xs[:, cc, ts * 128:(ts + 1) * 128], identity=ident[:])
                            return ins
                        op("tensor", tr, reads=[b_xs, b_ident], writes=[b_ps[bank]])
                        eng = evac_eng()
                        op(eng, copy_op(eng, ost[o][:, c4 * 512:(c4 + 1) * 512], psum[bank][:]),
                           reads=[b_ps[bank]], writes=[b_ost[o], b_h2])
                    op("gpsimd", lambda e, o=o, t0=t0, ts=ts: e.dma_start(
                        out=y_out[t0 + ts * 128:t0 + (ts + 1) * 128, :], in_=ost[o][:]),
                       reads=[b_ost[o], b_h2], dma=b_ost[o])

    def run_phase(f, *a):
        P.phase_begin()
        f(*a)
        P.phase_end()
        conv["done"] = conv["issued"]

    def first_phase():
        prepass_w0()
        prepass_dft()

    run_phase(first_phase)
    run_phase(phase_T)
    for l in range(DEPTH):
        run_phase(phase_A, l)
        run_phase(phase_A2, l)
        run_phase(phase_B, l)
        run_phase(phase_C1, l)
        run_phase(phase_H, l)
        run_phase(phase_C2, l)

    with nc.Block() as block:
        P.emit(block)
    stack.close()
    return nc


def pack_pvec(cfg, inp, Bval):
    c = cfg
    pv = np.zeros((128, c.NPV), np.float32)

    def put(name, l, vec):
        n = vec.shape[0] // 128
        o = c.pv[(name, l)]
        pv[:, o:o + n] = vec.reshape(n, 128).T

    def putk(name, l, mat):
        K = mat.shape[0]
        n = mat.shape[1] // 128
        o = c.pv[(name, l)]
        pv[:, o:o + K * n] = mat.reshape(K, n, 128).transpose(2, 0, 1).reshape(128, K * n)

    for l in range(c.DEPTH):
        put("gmix", l, inp["norm_mix_g"][l])
        put("gffn", l, inp["norm_ffn_g"][l])
        putk("scw", l, inp["sc_conv_w"][l])
        putk("cfw", l, inp["cf_conv_w"][l])
        put("cfb", l, inp["cf_conv_b"][l])
        put("lng", l, inp["cf_ln_g"][l])
        put("lnb", l, inp["cf_ln_b"][l])
        putk("ffw", l, inp["ffn_conv_w"][l])
    put("gfin", 0, inp["final_norm_g"])
    pv[:, c.pv[("B", 0)]] = Bval
    return pv


def dft_factor_tables(cfg, connected):
    NTOK, NS = cfg.NTOK, cfg.NS
    S = NTOK if connected else cfg.SEG
    a = np.arange(NS, dtype=np.int64)
    k = np.arange(NTOK, dtype=np.int64)
    a_loc = a % (S // 128)
    k_loc = k % S
    same = ((a // (S // 128))[:, None] == (k // S)[None, :]).astype(np.float64)
    alpha = ((128 * a_loc[:, None] * k_loc[None, :]) % S).astype(np.float64) * (2.0 * np.pi / S)
    b = np.arange(128, dtype=np.int64)
    beta = ((b[:, None] * k_loc[None, :]) % S).astype(np.float64) * (2.0 * np.pi / S)
    sc = 1.0 / np.sqrt(S * 128.0)
    rowtab = np.zeros((2, 2, NS, NTOK), np.float32)
    rowtab[0, 0] = sc * np.cos(alpha) * same
    rowtab[0, 1] = -sc * np.sin(alpha) * same
    rowtab[1, 0] = -sc * np.sin(alpha) * same
    rowtab[1, 1] = -sc * np.cos(alpha) * same
    ptab = np.zeros((128, 2, NTOK), np.float32)
    ptab[:, 0, :] = np.cos(beta)
    ptab[:, 1, :] = np.sin(beta)
    return rowtab, ptab


def chan_dft():
    idx = (np.arange(128)[:, None] * np.arange(128)[None, :]) % 128
    ang = idx.astype(np.float64) * (2.0 * np.pi / 128)
    return np.concatenate([np.cos(ang), np.sin(ang)], axis=1).astype(np.float32)


def weight_unit(cfg, inp, key):
    c = cfg
    out = np.zeros((128, c.UE), np.float32)
    kind, l = key[0], key[1]
    if kind == "dn":
        gi, mq = key[2], key[3]
        j0, gn = c.groups[gi]
        blk = inp["w_down"][l][j0 * 128:(j0 + gn) * 128, mq * 512:(mq + 1) * 512]
        out[:, 0:gn * 512] = blk.reshape(gn, 128, 512).transpose(1, 0, 2).reshape(128, gn * 512)
        return out
    if kind == "in":
        src = c.win_units()[key[2]][2]
        blk = inp["w_in"][l][:, src * 128:(src + 1) * 128]
    elif kind == "out":
        m = key[2]
        blk = inp["w_out"][l][:, m * 128:(m + 1) * 128]
    else:
        j, h = key[2], key[3]
        col = h * c.DFF + j * 128
        blk = inp["w_up"][l][:, col:col + 128]
    out[:, 0:c.KD * 128] = blk.reshape(c.KD, 128, 128).transpose(1, 0, 2).reshape(128, c.KD * 128)
    return out


def weight_tiles(cfg, inp):
    c = cfg
    w = np.zeros((c.NU, 128, c.UE), np.float32)
    for g, key in enumerate(c.units):
        w[g] = weight_unit(c, inp, key)
    return w


_NC_CACHE = {}


def run_slots(cfg, slots, inp, trace=False):
    key = (cfg.D, cfg.NTOK, cfg.DEPTH)
    if key not in _NC_CACHE:
        _NC_CACHE[key] = build_program(cfg)
    nc = _NC_CACHE[key]
    tabs = {True: dft_factor_tables(cfg, True), False: dft_factor_tables(cfg, False)}
    pvs = {True: pack_pvec(cfg, inp, 1.0), False: pack_pvec(cfg, inp, 0.0)}
    cd = chan_dft()
    eye = np.eye(128, dtype=np.float32)
    wt = weight_tiles(cfg, inp)
    in_maps = []
    for r, (xs, conn) in enumerate(slots):
        in_maps.append({"x": xs, "wsh": wt, "pvec": pvs[conn], "rowtab": tabs[conn][0],
                        "ptab": tabs[conn][1], "cdft": cd, "ident": eye})
    res = run_bass_kernel_spmd(nc, in_maps, core_ids=list(range(cfg.NCORES)), trace=trace)
    return [r["y"] for r in res.results], res


def kernel(**inputs):
    cfg = Cfg()
    inp = {k: np.asarray(v) for k, v in inputs.items()}
    xp = np.ascontiguousarray(inp["x_prompt"], dtype=np.float32)
    xsm = np.ascontiguousarray(inp["x_sample"], dtype=np.float32)
    slots = [(xp[i], True) for i in range(4)]
    slots.append((xsm[0:2].reshape(cfg.NTOK, cfg.D), False))
    slots.append((xsm[2:4].reshape(cfg.NTOK, cfg.D), False))
    ys, _ = run_slots(cfg, slots, inp)
    y_prompt = np.stack(ys[0:4], axis=0)
    y_sample = np.concatenate([ys[4].reshape(2, cfg.SEG, cfg.D), ys[5].reshape(2, cfg.SEG, cfg.D)], axis=0)
    return (np.ascontiguousarray(y_prompt, dtype=np.float32), np.ascontiguousarray(y_sample, dtype=np.float32))
```

```python
import numpy as np
import concourse.bass as bass
import concourse.mybir as mybir
from concourse.bass_utils import run_bass_kernel_spmd

F32 = mybir.dt.float32
BF16 = mybir.dt.bfloat16
F32R = mybir.dt.float32r
AF = mybir.ActivationFunctionType
ALU = mybir.AluOpType
EPS = 1e-6
TT = 512
G = 8
SG = 2


class Cfg:
    def __init__(self, D=4096, NTOK=4096, DEPTH=2):
        self.D = D
        self.KD = D // 128
        nh = D // 128
        self.NF = nh // 4
        self.NSC = (nh - self.NF) // 2
        self.NCF = nh - self.NF - self.NSC
        self.NIN = 3 * self.NSC + self.NF + 2 * self.NCF
        self.DFF = ((8 * D // 3 + 255) // 256) * 256
        self.KF = self.DFF // 128
        self.NTOK = NTOK
        self.SEG = NTOK // 2
        self.NT = NTOK // TT
        self.NS = NTOK // 128
        self.DEPTH = DEPTH
        self.MQ = D // 512
        self.groups = [(s, min(G, self.KF - s)) for s in range(0, self.KF, G)]
        o = 0
        self.pv = {}
        for l in range(DEPTH):
            for name, n in (("gmix", self.KD), ("gffn", self.KD), ("scw", 3 * self.NSC),
                            ("cfw", 31 * self.NCF), ("cfb", self.NCF), ("lng", self.NCF),
                            ("lnb", self.NCF), ("ffw", 3 * self.KF)):
                self.pv[(name, l)] = o
                o += n
        self.pv[("gfin", 0)] = o
        o += self.KD
        self.pv[("B", 0)] = o
        o += 1
        self.NPV = o
        self.NCORES = 6
        self.UE = max(self.KD * 128, G * 512)
        self.units = []
        for l in range(DEPTH):
            for u in range(self.NIN):
                self.units.append(("in", l, u))
            for m in range(self.KD):
                self.units.append(("out", l, m))
            for j in range(self.KF):
                for h in range(2):
                    self.units.append(("up", l, j, h))
            for gi in range(len(self.groups)):
                for mq in range(self.MQ):
                    self.units.append(("dn", l, gi, mq))
        self.uid = {k: i for i, k in enumerate(self.units)}
        self.NU = len(self.units)

    def win_units(self):
        c = self
        u = []
        for j in range(c.NSC):
            u += [("c", j, c.NSC + j), ("v", j, 2 * c.NSC + j), ("b", j, j)]
        for j in range(c.NF):
            u.append(("f", j, 3 * c.NSC + j))
        for j in range(c.NCF):
            u += [("cg", j, 3 * c.NSC + c.NF + c.NCF + j), ("cu", j, 3 * c.NSC + c.NF + j)]
        return u


class Sem:
    def __init__(self, h, name):
        self.h = h
        self.name = name
        self.count = 0


class Buf:
    def __init__(self, name, sem=None):
        self.name = name
        self.sem = sem
        self.last_w = None
        self.readers = []


class Eng:
    def __init__(self, name, sem):
        self.name = name
        self.sem = sem
        self.ops = []
        self.waited = {}


class Prog:
    def __init__(self, nc, stack):
        self.nc = nc
        self.stack = stack
        self.sems = []
        self.eng = {}
        for n in ("tensor", "vector", "scalar", "gpsimd", "sync"):
            self.eng[n] = Eng(n, self.new_sem("done_" + n))
        self.nbuf = 0
        self.free_sems = []
        self.phase_sems = None

    def new_sem(self, name):
        h = self.stack.enter_context(self.nc.semaphore(name))
        s = Sem(h, name)
        self.sems.append(s)
        return s

    def buf(self, name, dma=False):
        self.nbuf += 1
        if not dma:
            return Buf(name)
        if self.phase_sems is not None and self.free_sems:
            s = self.free_sems.pop()
        else:
            s = self.new_sem("b%d" % self.nbuf)
        if self.phase_sems is not None:
            self.phase_sems.append(s)
        return Buf(name, s)

    def phase_begin(self):
        self.phase_sems = []

    def phase_end(self):
        self.barrier()
        self.free_sems.extend(self.phase_sems)
        self.phase_sems = None

    def op(self, eng, fn, reads=(), writes=(), dma=None):
        e = self.eng[eng]
        need = {}

        def add(ev):
            if ev is None:
                return
            s, v = ev
            if need.get(s.name, (None, 0))[1] < v:
                need[s.name] = (s, v)

        for b in reads:
            add(b.last_w)
        for b in writes:
            add(b.last_w)
            for r in b.readers:
                add(r)
        waits = []
        for s, v in need.values():
            if s is e.sem and eng == "tensor" and dma is None:
                continue
            if e.waited.get(s.name, 0) >= v:
                continue
            e.waited[s.name] = v
            waits.append((s, v))
        if dma is not None:
            s = dma.sem
            s.count += 16
            inc = (s, 16)
        else:
            s = e.sem
            s.count += 1
            inc = (s, 1)
        ev = (s, s.count)
        e.ops.append((waits, fn, inc))
        for b in writes:
            b.last_w = ev
            b.readers = []
        for b in reads:
            if b not in writes:
                b.readers.append(ev)
        return ev

    def barrier(self):
        for e in self.eng.values():
            waits = []
            for s in self.sems:
                if s.count > e.waited.get(s.name, 0):
                    e.waited[s.name] = s.count
                    waits.append((s, s.count))
            if waits:
                e.ops.append((waits, None, None))

    def emit(self, block):
        def make(e):
            def body(be):
                for waits, fn, inc in e.ops:
                    for s, v in waits:
                        be.wait_ge(s.h, v)
                    if fn is not None:
                        ins = fn(be)
                        ins.then_inc(inc[0].h, inc[1])
            return body

        block.tensor(make(self.eng["tensor"]))
        block.vector(make(self.eng["vector"]))
        block.scalar(make(self.eng["scalar"]))
        block.gpsimd(make(self.eng["gpsimd"]))
        block.sync(make(self.eng["sync"]))


class Arena:
    def __init__(self, nc, base, limit):
        self.nc = nc
        self.base = base
        self.off = base
        self.limit = limit
        self.n = 0

    def reset(self):
        self.off = self.base

    def alloc(self, name, shape, dtype):
        size = int(np.prod(shape[1:])) * (4 if dtype == F32 else 2)
        size = (size + 63) // 64 * 64
        assert self.off + size <= self.limit, (name, self.off, size, self.limit)
        self.n += 1
        t = self.nc.alloc_sbuf_tensor_at("%s_%d" % (name, self.n), list(shape), dtype, offset=self.off)
        self.last_off = self.off
        self.off += size
        return t

    def alloc_at(self, name, shape, dtype, off):
        self.n += 1
        return self.nc.alloc_sbuf_tensor_at("%s_%d" % (name, self.n), list(shape), dtype, offset=off)


def build_program(cfg):
    from contextlib import ExitStack

    c = cfg
    D, KD, NTOK, NT, NS, KF, DEPTH = c.D, c.KD, c.NTOK, c.NT, c.NS, c.KF, c.DEPTH
    NSC, NF, NCF, NIN, MQ, UE = c.NSC, c.NF, c.NCF, c.NIN, c.MQ, c.UE
    nc = bass.Bass("TRN2", target_bir_lowering=False)

    def din(name, shape):
        return nc.dram_tensor(name, list(shape), F32, kind="ExternalInput").ap()

    x_in = din("x", (NTOK, D))
    wsh_in = din("wsh", (c.NU, 128, UE))
    pvec_in = din("pvec", (128, c.NPV))
    rowtab_in = din("rowtab", (2, 2, NS, NTOK))
    ptab_in = din("ptab", (128, 2, NTOK))
    cdft_in = din("cdft", (128, 256))
    ident_in = din("ident", (128, 128))
    y_out = nc.dram_tensor("y", [NTOK, D], F32, kind="ExternalOutput").ap()

    def dscr(name, shape, dt):
        return nc.dram_tensor(name, list(shape), dt).ap()

    WCH = 192
    walls = [dscr("wall%d" % i, (min(WCH, c.NU - i * WCH), 128, UE), BF16) for i in range((c.NU + WCH - 1) // WCH)]
    uid = c.uid

    def wunit(u):
        return walls[u // WCH][u % WCH]

    conv = {"issued": 0, "done": 0}

    def wsrc(key, n=KD * 128):
        assert uid[key] < conv["done"], ("weight unit used before its bf16 conversion completed", key)
        return wunit(uid[key])[:, 0:n]

    dftb = dscr("dftb", (2, NT, 128, NS * 512), BF16)
    xT = dscr("xT", (KD, 128, NTOK), F32)
    h2T = dscr("h2T", (KD, 128, NTOK), BF16)
    yT = dscr("yT", (KD, 128, NTOK), BF16)
    cvT = dscr("cvT", (NSC, 128, NTOK), F32)
    bT = dscr("bT", (NSC, 128, NTOK), F32)
    ggT = dscr("ggT", (NCF, 128, NTOK), BF16)
    ghT = dscr("ghT", (NF, 128, NS, 256), BF16)

    stack = ExitStack()
    P = Prog(nc, stack)
    op = P.op

    SB_BASE = 16640
    SB_LIMIT = 229376 - 64
    pers = Arena(nc, SB_BASE, SB_LIMIT)
    ones_f = pers.alloc("ones", (128, 128), F32)
    ones_r = pers.alloc("onesr", (128, 128), F32)
    ident = pers.alloc("ident", (128, 128), F32)
    pvec = pers.alloc("pvec", (128, c.NPV), F32)
    cd_f = pers.alloc("cdf", (128, 256), F32)
    cd_b = pers.alloc("cdb", (128, 256), BF16)
    identb = pers.alloc("identb", (128, 128), BF16)
    hh = pers.alloc("hh", (128, KD, 2 * NT), BF16)
    ghalo = pers.alloc("ghalo", (128, KF, 2 * NT), F32)
    ghl = pers.alloc("ghl", (128, KF, NT), F32)
    ghr = pers.alloc("ghr", (128, KF, NT), F32)
    ar = Arena(nc, pers.off, SB_LIMIT)

    b_ones, b_ident, b_cdb = P.buf("ones"), P.buf("ident", True), P.buf("cdb")
    b_pvec, b_cdf = P.buf("pvec", True), P.buf("cdf", True)
    b_hh, b_ghalo, b_ghl, b_ghr = P.buf("hh"), P.buf("ghalo"), P.buf("ghl"), P.buf("ghr")

    psum = [stack.enter_context(nc.psum_tensor("ps%d" % i, [128, 512], F32)) for i in range(8)]
    b_ps = [P.buf("ps%d" % i) for i in range(8)]

    def pcol(name, l, j):
        o = c.pv[(name, l)] + j
        return pvec[:, o:o + 1]

    Bcol = pcol("B", 0, 0)

    op("gpsimd", lambda e: e.dma_start(out=pvec[:], in_=pvec_in), writes=[b_pvec], dma=b_pvec)
    op("gpsimd", lambda e: e.dma_start(out=ident[:], in_=ident_in), writes=[b_ident], dma=b_ident)
    op("gpsimd", lambda e: e.dma_start(out=cd_f[:], in_=cdft_in), writes=[b_cdf], dma=b_cdf)
    op("vector", lambda e: e.memset(ones_f[:], 1.0), writes=[b_ones])
    b_onesr = P.buf("onesr")
    op("vector", lambda e: e.tensor_copy(out=ones_r[:].bitcast(F32R), in_=ones_f[:]), reads=[b_ones], writes=[b_onesr])
    op("vector", lambda e: e.tensor_copy(out=cd_b[:], in_=cd_f[:]), reads=[b_cdf], writes=[b_cdb])
    b_identb = P.buf("identb")
    op("vector", lambda e: e.tensor_copy(out=identb[:], in_=ident[:]), reads=[b_ident], writes=[b_identb])
    rr = {"n": 0}

    def evac_eng():
        rr["n"] += 1
        return "vector" if rr["n"] % 2 else "scalar"

    def copy_op(eng, out, in_):
        if eng == "scalar":
            return lambda e: e.activation(out=out, in_=in_, func=AF.Copy)
        return lambda e: e.tensor_copy(out=out, in_=in_)

    b_bg = P.buf("bgconv", True)

    def bg(n):
        for _ in range(n):
            u = conv["issued"]
            if u >= c.NU:
                return
            conv["issued"] += 1
            op("gpsimd", lambda e, u=u: e.dma_start(out=wunit(u), in_=wsh_in[u]), dma=b_bg)

    def prepass_w0():
        bg(NIN)

    def spread(total, nslices, i):
        return ((i + 1) * total) // nslices - (i * total) // nslices

    def prepass_dft():
        ar.reset()
        ptab = ar.alloc("ptab", (128, 2, NTOK), F32)
        b_ptab = P.buf("ptab", True)
        bc = [[ar.alloc("bc", (128, 8, 512), F32) for _ in range(2)] for _ in range(2)]
        b_bc = [[P.buf("bc", True) for _ in range(2)] for _ in range(2)]
        m = [ar.alloc("m", (128, 8, 512), F32) for _ in range(2)]
        b_m = [P.buf("m") for _ in range(2)]
        ob = [ar.alloc("ob", (128, 8, 512), BF16) for _ in range(2)]
        b_ob = [P.buf("ob", True) for _ in range(2)]
        op("sync", lambda e: e.dma_start(out=ptab[:], in_=ptab_in), writes=[b_ptab], dma=b_ptab)
        n = 0
        for t in range(2):
            for kt in range(NT):
                for sq in range(NS // 8):
                    s = n % 2
                    n += 1
                    for q in range(2):
                        op("sync", lambda e, s=s, q=q, t=t, kt=kt, sq=sq: e.dma_start(
                            out=bc[s][q][:],
                            in_=rowtab_in[t, q, sq * 8:(sq + 1) * 8, kt * 512:(kt + 1) * 512].partition_broadcast(128)),
                           writes=[b_bc[s][q]], dma=b_bc[s][q])
                        op("vector", lambda e, s=s, q=q, kt=kt: e.tensor_tensor(
                            out=m[q][:], in0=bc[s][q][:],
                            in1=ptab[:, q, kt * 512:(kt + 1) * 512][:, None, :].to_broadcast([128, 8, 512]),
                            op=ALU.mult), reads=[b_bc[s][q], b_ptab], writes=[b_m[q]])
                    op("vector", lambda e, s=s: e.tensor_tensor(out=ob[s][:], in0=m[0][:], in1=m[1][:], op=ALU.add),
                       reads=[b_m[0], b_m[1]], writes=[b_ob[s]])
                    op("gpsimd", lambda e, s=s, t=t, kt=kt, sq=sq: e.dma_start(
                        out=dftb[t, kt, :, sq * 4096:(sq + 1) * 4096], in_=ob[s][:].rearrange("p a k -> p (a k)")),
                       reads=[b_ob[s]], dma=b_ob[s])

    def phase_T():
        ar.reset()
        xin = [ar.alloc("xin", (128, D), F32) for _ in range(2)]
        xst = [ar.alloc("xst", (128, KD, 128), F32) for _ in range(2)]
        b_xin = [P.buf("xin", True) for _ in range(2)]
        b_xst = [P.buf("xst", True) for _ in range(2)]
        pb = 0
        for tb in range(NS):
            s = tb % 2
            op("sync", lambda e, s=s, tb=tb: e.dma_start(out=xin[s][:], in_=x_in[tb * 128:(tb + 1) * 128, :]),
               writes=[b_xin[s]], dma=b_xin[s])
            bg(spread(KD, NS, tb))
            for c4 in range(KD // 4):
                bank = pb % 4
                pb += 1

                def tr(e, s=s, c4=c4, bank=bank):
                    for q in range(4):
                        cc = c4 * 4 + q
                        ins = e.transpose(out=psum[bank][:, q * 128:(q + 1) * 128],
                                          in_=xin[s][:, cc * 128:(cc + 1) * 128], identity=ident[:])
                    return ins
                op("tensor", tr, reads=[b_xin[s], b_ident], writes=[b_ps[bank]])
                eng = evac_eng()
                op(eng, copy_op(eng, xst[s][:, c4 * 4:(c4 + 1) * 4, :],
                                psum[bank][:].rearrange("p (q t) -> p q t", q=4)),
                   reads=[b_ps[bank]], writes=[b_xst[s]])
            op("gpsimd", lambda e, s=s, tb=tb: e.dma_start(
                out=xT[:, :, tb * 128:(tb + 1) * 128].rearrange("c p t -> p c t"), in_=xst[s][:]),
               reads=[b_xst[s]], dma=b_xst[s])

    class WRing:
        def __init__(self, name, n, elems):
            self.t = [ar.alloc(name, (128, elems), BF16) for _ in range(n)]
            self.b = [P.buf(name, True) for _ in range(n)]
            self.i = 0
            self.n = n

        def load(self, src, elems=None):
            s = self.i % self.n
            self.i += 1
            t = self.t[s]
            if elems is None:
                op("sync", lambda e: e.dma_start(out=t[:], in_=src), writes=[self.b[s]], dma=self.b[s])
            else:
                op("sync", lambda e: e.dma_start(out=t[:, 0:elems], in_=src), writes=[self.b[s]], dma=self.b[s])
            return t, self.b[s]

    def norm_stats(xs, b_xs, sq, b_sq, rstd, b_rstd, nbank, n=TT):
        for cc in range(KD):
            s = cc % 2
            op("scalar", lambda e, cc=cc, s=s: e.activation(out=sq[s][:, 0:n].bitcast(F32R), in_=xs[:, cc, :],
                                                          func=AF.Square),
               reads=[b_xs], writes=[b_sq[s]])
            op("tensor", lambda e, cc=cc, s=s: e.matmul(psum[nbank][:, 0:n], lhsT=ones_r[:].bitcast(F32R),
                                                      rhs=sq[s][:, 0:n].bitcast(F32R),
                                                      start=(cc == 0), stop=(cc == KD - 1)),
               reads=[b_sq[s], b_ones], writes=[b_ps[nbank]])
        op("vector", lambda e: e.tensor_scalar(out=rstd[:, 0:n], in0=psum[nbank][:, 0:n], scalar1=1.0 / D,
                                               scalar2=EPS, op0=ALU.mult, op1=ALU.add),
           reads=[b_ps[nbank]], writes=[b_rstd])
        op("scalar", lambda e: e.activation(out=rstd[:, 0:n], in_=rstd[:, 0:n], func=AF.Sqrt),
           reads=[b_rstd], writes=[b_rstd])
        op("vector", lambda e: e.reciprocal(out=rstd[:, 0:n], in_=rstd[:, 0:n]), reads=[b_rstd], writes=[b_rstd])

    def norm_apply(xs, b_xs, gname, l, out, b_out, rstd, b_rstd, n=TT):
        for cc in range(KD):
            op("vector", lambda e, cc=cc: e.scalar_tensor_tensor(
                out=out[:, cc, :], in0=xs[:, cc, :], scalar=pcol(gname, l, cc), in1=rstd[:, 0:n],
                op0=ALU.mult, op1=ALU.mult),
               reads=[b_xs, b_rstd, b_pvec], writes=[b_out])

    def rmsnorm_tile(xs, b_xs, gname, l, out, b_out, sq, b_sq, rstd, b_rstd, nbank, n=TT):
        norm_stats(xs, b_xs, sq, b_sq, rstd, b_rstd, nbank, n)
        norm_apply(xs, b_xs, gname, l, out, b_out, rstd, b_rstd, n)

    def mm_group(bank, wt, b_w, act, b_act, nk, n=512, woff=0):
        def f(e):
            for k in range(nk):
                ins = e.matmul(psum[bank][:, 0:n], lhsT=wt[:, woff + k * 128: woff + (k + 1) * 128],
                               rhs=act[:, k, 0:n], start=(k == 0), stop=(k == nk - 1))
            return ins
        op("tensor", f, reads=[b_w, b_act], writes=[b_ps[bank]])

    def phase_A(l):
        ar.reset()
        xs = ar.alloc("xs", (128, KD, TT), F32)
        h = ar.alloc("h", (128, KD, TT), BF16)
        sq = [ar.alloc("sq", (128, TT), F32) for _ in range(2)]
        rstd2 = [ar.alloc("rstd", (128, TT), F32) for _ in range(2)]
        b_rstd2 = [P.buf("rstd") for _ in range(2)]
        tmp = [ar.alloc("tmp", (128, TT), F32) for _ in range(2)]
        stg = {k: [ar.alloc("stg" + k, (128, SG, TT), BF16 if k == "gg" else F32) for _ in range(2)]
               for k in ("cv", "b", "gg")}
        fb = [ar.alloc("fb", (128, TT), BF16) for _ in range(2)]
        ghst = ar.alloc("ghst", (128, NF, 4, 256), BF16)
        wr = WRing("wA", 3, KD * 128)
        b_xs, b_h = P.buf("xs", True), P.buf("h")
        b_sq = [P.buf("sq") for _ in range(2)]
        b_tmp = [P.buf("tmp") for _ in range(2)]
        b_stg = {k: [P.buf("stg" + k, True) for _ in range(2)] for k in ("cv", "b", "gg")}

        def load_x(i):
            t0 = i * TT
            op("sync", lambda e: e.dma_start(out=xs[:], in_=xT[:, :, t0:t0 + TT].rearrange("c p t -> p c t")),
               writes=[b_xs], dma=b_xs)

        load_x(0)
        norm_stats(xs, b_xs, sq, b_sq, rstd2[0], b_rstd2[0], 0)
        b_fb = [P.buf("fb") for _ in range(2)]
        b_ghst = P.buf("ghst", True)
        units = c.win_units()
        dstT = {"cv": cvT, "b": bT, "gg": ggT}
        nchunks = {"cv": NSC, "b": NSC, "gg": NCF}
        for i in range(NT):
            t0 = i * TT
            norm_apply(xs, b_xs, "gmix", l, h, b_h, rstd2[i % 2], b_rstd2[i % 2])
            if i + 1 < NT:
                load_x(i + 1)
            pbank = 0
            tcount = 0
            ghbank = 0

            def stage_slot(kind, j):
                return stg[kind][(j // SG) % 2], b_stg[kind][(j // SG) % 2]

            def maybe_store(kind, j, t0=t0):
                n = nchunks[kind]
                if j % SG == SG - 1 or j == n - 1:
                    j0 = (j // SG) * SG
                    cnt = j - j0 + 1
                    st, bst = stage_slot(kind, j)
                    op("gpsimd", lambda e: e.dma_start(
                        out=dstT[kind][j0:j0 + cnt, :, t0:t0 + TT].rearrange("c p t -> p c t"),
                        in_=st[:, 0:cnt, :]), reads=[bst], dma=bst)

            for u, (kind, j, _) in enumerate(units):
                if l == 0:
                    bg(spread(spread(2 * KF, NT, i), NIN, u))
                if u == NIN // 3 and i + 1 < NT:
                    norm_stats(xs, b_xs, sq, b_sq, rstd2[(i + 1) % 2], b_rstd2[(i + 1) % 2], 0)
                wt, b_w = wr.load(wsrc(("in", l, u)))
                bank = 1 + pbank % 4
                pbank += 1
                mm_group(bank, wt, b_w, h, b_h, KD)
                ps = psum[bank]
                bps = b_ps[bank]
                if kind == "c":
                    ts = tcount % 2
                    tcount += 1
                    op("scalar", copy_op("scalar", tmp[ts][:], ps[:]), reads=[bps], writes=[b_tmp[ts]])
                    last_tmp = ts
                elif kind == "v":
                    st, bst = stage_slot("cv", j)
                    op("vector", lambda e, st=st, j=j, ps=ps, ts=last_tmp: e.tensor_tensor(
                        out=st[:, j % SG, :], in0=tmp[ts][:], in1=ps[:], op=ALU.mult),
                       reads=[bps, b_tmp[last_tmp]], writes=[bst])
                    maybe_store("cv", j)
                elif kind == "b":
                    st, bst = stage_slot("b", j)
                    op("scalar", copy_op("scalar", st[:, j % SG, :], ps[:]), reads=[bps], writes=[bst])
                    maybe_store("b", j)
                elif kind == "f":
                    s = j % 2
                    op("scalar", copy_op("scalar", fb[s][:], ps[:]), reads=[bps], writes=[b_fb[s]])
                    for half in range(2):
                        gb = 5 + ghbank % 2
                        ghbank += 1

                        def chmm(e, s=s, half=half, gb=gb):
                            for q in range(2):
                                ts_ = half * 2 + q
                                ins = e.matmul(psum[gb][:, q * 256:(q + 1) * 256],
                                               lhsT=fb[s][:, ts_ * 128:(ts_ + 1) * 128], rhs=cd_b[:],
                                               start=True, stop=True)
                            return ins
                        op("tensor", chmm, reads=[b_fb[s], b_cdb], writes=[b_ps[gb]])
                        op("vector", lambda e, j=j, half=half, gb=gb: e.tensor_copy(
                            out=ghst[:, j, half * 2:half * 2 + 2, :],
                            in_=psum[gb][:].rearrange("p (q c) -> p q c", q=2)),
                           reads=[b_ps[gb]], writes=[b_ghst])
                    if j == NF - 1:
                        op("gpsimd", lambda e, i=i: e.dma_start(
                            out=ghT[:, :, i * 4:(i + 1) * 4, :].rearrange("j p s c -> p j s c"), in_=ghst[:]),
                           reads=[b_ghst], dma=b_ghst)
                elif kind == "cg":
                    ts = tcount % 2
                    tcount += 1
                    op("scalar", lambda e, ts=ts, ps=ps: e.activation(out=tmp[ts][:], in_=ps[:], func=AF.Sigmoid),
                       reads=[bps], writes=[b_tmp[ts]])
                    last_tmp = ts
                elif kind == "cu":
                    st, bst = stage_slot("gg", j)
                    op("vector", lambda e, st=st, j=j, ps=ps, ts=last_tmp: e.tensor_tensor(
                        out=st[:, j % SG, :], in0=tmp[ts][:], in1=ps[:], op=ALU.mult),
                       reads=[bps, b_tmp[last_tmp]], writes=[bst])
                    maybe_store("gg", j)

    def phase_A2(l):
        ar.reset()
        cvws = [ar.alloc("cvw", (128, NSC, TT + 2), F32) for _ in range(2)]
        bw = ar.alloc("bw", (128, NSC, TT), F32)
        ggws = [ar.alloc("ggw", (128, NCF, TT + 30), BF16) for _ in range(2)]
        dg = [ar.alloc("dg", (128, 31, 128), BF16) for _ in range(2)]
        b_dg = [P.buf("dg") for _ in range(2)]
        co = ar.alloc("co", (128, NCF, TT), F32)
        acc = [ar.alloc("acc", (128, TT), F32) for _ in range(2)]
        sq = [ar.alloc("sq2", (128, TT), F32) for _ in range(2)]
        mean = ar.alloc("mean", (128, TT), F32)
        var = ar.alloc("var", (128, TT), F32)
        ysc = ar.alloc("ysc", (128, NSC, TT), BF16)
        ycf = ar.alloc("ycf", (128, NCF, TT), BF16)
        b_cvws = [P.buf("cvw", True) for _ in range(2)]
        b_ggws = [P.buf("ggw", True) for _ in range(2)]
        b_bw = P.buf("bw", True)
        b_co, b_mean, b_var = P.buf("co"), P.buf("mean"), P.buf("var")
        b_acc = [P.buf("acc") for _ in range(2)]
        b_sq = [P.buf("sq2") for _ in range(2)]
        b_ysc, b_ycf = P.buf("ysc", True), P.buf("ycf", True)
        DCF = NCF * 128

        def load_window(dst, b_dst, src, halo, i):
            t0 = i * TT
            lo = max(t0 - halo, 0)
            hi = min(t0 + TT + halo, NTOK)
            dlo = lo - (t0 - halo)
            if dlo > 0:
                op("vector", lambda e: e.memset(dst[:, :, 0:dlo], 0.0), writes=[b_dst])
            if hi < t0 + TT + halo:
                op("vector", lambda e: e.memset(dst[:, :, TT + 2 * halo - (t0 + TT + halo - hi):TT + 2 * halo], 0.0),
                   writes=[b_dst])
            op("sync", lambda e: e.dma_start(out=dst[:, :, dlo:dlo + (hi - lo)],
                                             in_=src[:, :, lo:hi].rearrange("c p t -> p c t")),
               writes=[b_dst], dma=b_dst)
            if i == NT // 2 - 1:
                op("vector", lambda e: e.tensor_scalar(out=dst[:, :, TT + halo:TT + 2 * halo],
                                                       in0=dst[:, :, TT + halo:TT + 2 * halo],
                                                       scalar1=Bcol, scalar2=None, op0=ALU.mult),
                   reads=[b_pvec], writes=[b_dst])
            if i == NT // 2:
                op("vector", lambda e: e.tensor_scalar(out=dst[:, :, 0:halo], in0=dst[:, :, 0:halo],
                                                       scalar1=Bcol, scalar2=None, op0=ALU.mult),
                   reads=[b_pvec], writes=[b_dst])

        def loads(i):
            load_window(cvws[i % 2], b_cvws[i % 2], cvT, 1, i)
            load_window(ggws[i % 2], b_ggws[i % 2], ggT, 15, i)

        def load_b(i):
            t0 = i * TT
            op("sync", lambda e: e.dma_start(out=bw[:], in_=bT[:, :, t0:t0 + TT].rearrange("c p t -> p c t")),
               writes=[b_bw], dma=b_bw)

        def do_tile(i, cvw, b_cvw, ggw, b_ggw):
            t0 = i * TT
            if l == 0:
                bg(spread(len(c.groups) * MQ, NT, i))
            if i + 1 < NT:
                loads(i + 1)
            def sc_chunk(j):
                a = j % 2
                w = [pcol("scw", l, k * NSC + j) for k in range(3)]
                op("scalar", lambda e, a=a, j=j, w=w: e.activation(out=acc[a][:], in_=cvw[:, j, 0:TT], func=AF.Identity,
                                                                 scale=w[0]),
                   reads=[b_cvw, b_pvec], writes=[b_acc[a]])
                for k in (1, 2):
                    op("vector", lambda e, a=a, j=j, w=w, k=k: e.scalar_tensor_tensor(
                        out=acc[a][:], in0=cvw[:, j, k:k + TT], scalar=w[k], in1=acc[a][:],
                        op0=ALU.mult, op1=ALU.add), reads=[b_cvw, b_pvec, b_acc[a]], writes=[b_acc[a]])
                op("vector", lambda e, a=a, j=j: e.tensor_tensor(out=ysc[:, j, :], in0=acc[a][:], in1=bw[:, j, :],
                                                               op=ALU.mult),
                   reads=[b_acc[a], b_bw], writes=[b_ysc])
            def sc_finish():
                op("gpsimd", lambda e: e.dma_start(out=yT[0:NSC, :, t0:t0 + TT].rearrange("c p t -> p c t"),
                                                   in_=ysc[:]), reads=[b_ysc], dma=b_ysc)
                if i + 1 < NT:
                    load_b(i + 1)
            for j in range(NCF):
                d = j % 2
                o = c.pv[("cfw", l)] + j
                wk = pvec[:, o:o + 31 * NCF:NCF]
                op("vector", lambda e, d=d, wk=wk: e.tensor_tensor(
                    out=dg[d][:], in0=identb[:, None, :].to_broadcast([128, 31, 128]),
                    in1=wk[:, :, None].to_broadcast([128, 31, 128]), op=ALU.mult),
                   reads=[b_identb, b_pvec], writes=[b_dg[d]])
                if j < NSC:
                    sc_chunk(j)
                    if j == NSC - 1:
                        sc_finish()
                cb = 2 + j % 4

                def cmm(e, d=d, j=j, cb=cb):
                    for k in range(31):
                        ins = e.matmul(psum[cb][:], lhsT=dg[d][:, k, :], rhs=ggw[:, j, k:k + TT],
                                       start=(k == 0), stop=(k == 30))
                    return ins
                op("tensor", cmm, reads=[b_dg[d], b_ggw], writes=[b_ps[cb]])
                if j % 2 == 0:
                    op("scalar", lambda e, j=j, cb=cb: e.activation(out=co[:, j, :], in_=psum[cb][:], func=AF.Identity,
                                                                  bias=pcol("cfb", l, j)),
                       reads=[b_ps[cb], b_pvec], writes=[b_co])
                else:
                    op("vector", lambda e, j=j, cb=cb: e.tensor_scalar(out=co[:, j, :], in0=psum[cb][:],
                                                                     scalar1=pcol("cfb", l, j), scalar2=None,
                                                                     op0=ALU.add),
                       reads=[b_ps[cb], b_pvec], writes=[b_co])
                s = j % 2
                op("scalar", lambda e, j=j, s=s: e.activation(out=sq[s][:].bitcast(F32R), in_=co[:, j, :],
                                                            func=AF.Square),
                   reads=[b_co], writes=[b_sq[s]])
                op("tensor", lambda e, j=j: e.matmul(psum[0][:], lhsT=ones_f[:], rhs=co[:, j, :],
                                                     start=(j == 0), stop=(j == NCF - 1)),
                   reads=[b_co, b_ones], writes=[b_ps[0]])
                op("tensor", lambda e, j=j, s=s: e.matmul(psum[1][:], lhsT=ones_r[:].bitcast(F32R),
                                                        rhs=sq[s][:].bitcast(F32R),
                                                        start=(j == 0), stop=(j == NCF - 1)),
                   reads=[b_sq[s], b_ones], writes=[b_ps[1]])
            for j in range(NCF, NSC):
                sc_chunk(j)
                if j == NSC - 1:
                    sc_finish()
            op("scalar", lambda e: e.activation(out=mean[:], in_=psum[0][:], func=AF.Identity, scale=1.0 / DCF),
               reads=[b_ps[0]], writes=[b_mean])
            op("vector", lambda e: e.tensor_tensor(out=var[:], in0=mean[:], in1=mean[:], op=ALU.mult),
               reads=[b_mean], writes=[b_var])
            op("vector", lambda e: e.scalar_tensor_tensor(out=var[:], in0=psum[1][:], scalar=1.0 / DCF, in1=var[:],
                                                          op0=ALU.mult, op1=ALU.subtract),
               reads=[b_ps[1], b_var], writes=[b_var])
            op("vector", lambda e: e.tensor_scalar(out=var[:], in0=var[:], scalar1=EPS, scalar2=None,
                                                   op0=ALU.add), reads=[b_var], writes=[b_var])
            op("scalar", lambda e: e.activation(out=var[:], in_=var[:], func=AF.Sqrt),
               reads=[b_var], writes=[b_var])
            op("vector", lambda e: e.reciprocal(out=var[:], in_=var[:]), reads=[b_var], writes=[b_var])
            for j in range(NCF):
                a = j % 2
                op("vector", lambda e, j=j, a=a: e.tensor_tensor(out=acc[a][:], in0=co[:, j, :], in1=mean[:],
                                                               op=ALU.subtract),
                   reads=[b_co, b_mean], writes=[b_acc[a]])
                op("vector", lambda e, a=a: e.tensor_tensor(out=acc[a][:], in0=acc[a][:], in1=var[:], op=ALU.mult),
                   reads=[b_acc[a], b_var], writes=[b_acc[a]])
                op("scalar", lambda e, j=j, a=a: e.activation(out=ycf[:, j, :], in_=acc[a][:], func=AF.Silu,
                                                            bias=pcol("lnb", l, j), scale=pcol("lng", l, j)),
                   reads=[b_acc[a], b_pvec], writes=[b_ycf])
            op("gpsimd", lambda e, t0=t0: e.dma_start(
                out=yT[NSC + NF:NSC + NF + NCF, :, t0:t0 + TT].rearrange("c p t -> p c t"), in_=ycf[:]),
               reads=[b_ycf], dma=b_ycf)

        loads(0)
        load_b(0)
        for i in range(NT):
            do_tile(i, cvws[i % 2], b_cvws[i % 2], ggws[i % 2], b_ggws[i % 2])


    def phase_B(l):
        ar.reset()
        tab = [[ar.alloc("tab", (128, NS, 512), BF16) for _ in range(2)] for _ in range(2)]
        b_tab = [[P.buf("tab", True) for _ in range(2)] for _ in range(2)]
        gh = [ar.alloc("gh", (128, NS, 256), BF16) for _ in range(2)]
        b_gh = [P.buf("gh", True) for _ in range(2)]
        yf = [ar.alloc("yf", (128, NF, 512), BF16) for _ in range(2)]
        b_yf = [P.buf("yf", True) for _ in range(2)]
        gcount = 0
        pb = 0
        for kt in range(NT):
            s = kt % 2
            for t in range(2):
                op("sync", lambda e, s=s, t=t, kt=kt: e.dma_start(
                    out=tab[s][t][:].rearrange("p s k -> p (s k)"), in_=dftb[t, kt]),
                   writes=[b_tab[s][t]], dma=b_tab[s][t])
            for g in range(NF):
                gs = gcount % 2
                gcount += 1
                op("gpsimd", lambda e, gs=gs, g=g: e.dma_start(out=gh[gs][:], in_=ghT[g]),
                   writes=[b_gh[gs]], dma=b_gh[gs])
                bank = 2 + pb % 4
                pb += 1

                def dmm(e, s=s, gs=gs, bank=bank):
                    n = 0
                    for t in range(2):
                        for sc in range(NS):
                            ins = e.matmul(psum[bank][:], lhsT=gh[gs][:, sc, t * 128:(t + 1) * 128],
                                           rhs=tab[s][t][:, sc, :], start=(n == 0), stop=(n == 2 * NS - 1))
                            n += 1
                    return ins
                op("tensor", dmm, reads=[b_gh[gs], b_tab[s][0], b_tab[s][1]], writes=[b_ps[bank]])
                eng = evac_eng()
                op(eng, copy_op(eng, yf[s][:, g, :], psum[bank][:]), reads=[b_ps[bank]], writes=[b_yf[s]])
            op("gpsimd", lambda e, s=s, kt=kt: e.dma_start(
                out=yT[NSC:NSC + NF, :, kt * 512:(kt + 1) * 512].rearrange("c p t -> p c t"), in_=yf[s][:]),
               reads=[b_yf[s]], dma=b_yf[s])

    def phase_C1(l):
        ar.reset()
        TH = TT // 2
        NST = 2 * NT
        xs = [ar.alloc("xs", (128, KD, TH), F32) for _ in range(2)]
        yt = [ar.alloc("yt", (128, KD, TH), BF16) for _ in range(2)]
        h2 = [ar.alloc("h2", (128, KD, TH), BF16) for _ in range(2)]
        sq = [ar.alloc("sq", (128, TH), F32) for _ in range(2)]
        rstd = ar.alloc("rstd", (128, TH), F32)
        wr = WRing("wC1", 4, KD * 128)
        b_xs = [P.buf("xs", True) for _ in range(2)]
        b_yt = [P.buf("yt", True) for _ in range(2)]
        b_h2 = [P.buf("h2", True) for _ in range(2)]
        b_rstd = P.buf("rstd")
        b_sq = [P.buf("sq") for _ in range(2)]
        pb = 0

        def loads(st):
            p = st % 2
            t0 = st * TH
            op("sync", lambda e: e.dma_start(out=xs[p][:], in_=xT[:, :, t0:t0 + TH].rearrange("c p t -> p c t")),
               writes=[b_xs[p]], dma=b_xs[p])
            op("sync", lambda e: e.dma_start(out=yt[p][:], in_=yT[:, :, t0:t0 + TH].rearrange("c p t -> p c t")),
               writes=[b_yt[p]], dma=b_yt[p])

        loads(0)
        for st in range(NST):
            p = st % 2
            t0 = st * TH
            i = st // 2
            for m in range(KD):
                if m == KD // 4 and st + 1 < NST:
                    loads(st + 1)
                wt, b_w = wr.load(wsrc(("out", l, m)))
                bank = 1 + pb % 4
                pb += 1
                mm_group(bank, wt, b_w, yt[p], b_yt[p], KD, n=TH)
                op("vector", lambda e, m=m, bank=bank, p=p: e.tensor_tensor(
                    out=xs[p][:, m, :], in0=xs[p][:, m, :], in1=psum[bank][:, 0:TH], op=ALU.add),
                   reads=[b_ps[bank], b_xs[p]], writes=[b_xs[p]])
            rmsnorm_tile(xs[p], b_xs[p], "gffn", l, h2[p], b_h2[p], sq, b_sq, rstd, b_rstd, 0, n=TH)
            if st % 2 == 0:
                op("vector", lambda e, i=i, p=p: e.tensor_copy(out=hh[:, :, 2 * i:2 * i + 1], in_=h2[p][:, :, 0:1]),
                   reads=[b_h2[p]], writes=[b_hh])
            else:
                op("vector", lambda e, i=i, p=p: e.tensor_copy(out=hh[:, :, 2 * i + 1:2 * i + 2],
                                                             in_=h2[p][:, :, TH - 1:TH]),
                   reads=[b_h2[p]], writes=[b_hh])
            op("gpsimd", lambda e, t0=t0, p=p: e.dma_start(out=xT[:, :, t0:t0 + TH].rearrange("c p t -> p c t"),
                                                         in_=xs[p][:]), reads=[b_xs[p]], dma=b_xs[p])
            op("gpsimd", lambda e, t0=t0, p=p: e.dma_start(out=h2T[:, :, t0:t0 + TH].rearrange("c p t -> p c t"),
                                                         in_=h2[p][:]), reads=[b_h2[p]], dma=b_h2[p])

    def phase_H(l):
        ar.reset()
        wr = WRing("wH", 3, KD * 128)
        NH = 2 * NT
        pb = 0
        for j in range(KF):
            wt, b_w = wr.load(wsrc(("up", l, j, 0)))
            bank = 1 + pb % 4
            pb += 1
            mm_group(bank, wt, b_w, hh, b_hh, KD, n=NH)
            eng = evac_eng()
            op(eng, copy_op(eng, ghalo[:, j, :], psum[bank][:, 0:NH]), reads=[b_ps[bank]], writes=[b_ghalo])
        gv = ghalo[:].rearrange("p j (i two) -> p j i two", two=2)
        op("vector", lambda e: e.memset(ghl[:, :, 0:1], 0.0), writes=[b_ghl])
        op("vector", lambda e: e.tensor_copy(out=ghl[:, :, 1:NT], in_=gv[:, :, 0:NT - 1, 1]),
           reads=[b_ghalo], writes=[b_ghl])
        op("vector", lambda e: e.memset(ghr[:, :, NT - 1:NT], 0.0), writes=[b_ghr])
        op("vector", lambda e: e.tensor_copy(out=ghr[:, :, 0:NT - 1], in_=gv[:, :, 1:NT, 0]),
           reads=[b_ghalo], writes=[b_ghr])
        mid = NT // 2
        op("vector", lambda e: e.tensor_scalar(out=ghl[:, :, mid:mid + 1], in0=ghl[:, :, mid:mid + 1],
                                               scalar1=Bcol, scalar2=None, op0=ALU.mult),
           reads=[b_pvec, b_ghl], writes=[b_ghl])
        op("vector", lambda e: e.tensor_scalar(out=ghr[:, :, mid - 1:mid], in0=ghr[:, :, mid - 1:mid],
                                               scalar1=Bcol, scalar2=None, op0=ALU.mult),
           reads=[b_pvec, b_ghr], writes=[b_ghr])

    def phase_C2(l):
        ar.reset()
        last = (l == DEPTH - 1)
        xs = ar.alloc("xs", (128, KD, TT), F32)
        h2 = ar.alloc("h2", (128, KD, TT), BF16)
        h2_off = ar.last_off
        aa = [ar.alloc("a", (128, G, TT), BF16) for _ in range(2)]
        acc = [ar.alloc("acc", (128, TT), F32) for _ in range(2)]
        sil = [ar.alloc("sil", (128, TT), F32) for _ in range(2)]
        wu = WRing("wU", 4, KD * 128)
        wd = WRing("wD", 3, G * 512)
        b_xs, b_h2 = P.buf("xs", True), P.buf("h2", True)
        b_aa = [P.buf("a") for _ in range(2)]
        b_acc = [P.buf("acc") for _ in range(2)]
        b_sil = [P.buf("sil") for _ in range(2)]
        if last:
            sq = [ar.alloc("sqf", (128, TT), F32) for _ in range(2)]
            b_sq = [P.buf("sqf") for _ in range(2)]
            rstd = sil[0]
            b_rstd = b_sil[0]
            ost = [ar.alloc_at("ost", (128, D), F32, h2_off + o_ * (D * 2)) for o_ in range(2)]
            assert 2 * D * 4 <= KD * TT * 2 + 2 * D * 2
            b_ost = [P.buf("ost", True) for _ in range(2)]
        cnt = {"gv": 0, "d": 0, "k": 0, "o": 0}

        def up_group(i, gi):
            j0, gn = c.groups[gi]
            a_s = gi % 2
            for jj in range(gn):
                j = j0 + jj
                wg, b_wg = wu.load(wsrc(("up", l, j, 0)))
                wv, b_wv = wu.load(wsrc(("up", l, j, 1)))
                s = cnt["gv"] % 2
                cnt["gv"] += 1
                bg, bv = 0 + s, 2 + s
                mm_group(bg, wg, b_wg, h2, b_h2, KD)
                mm_group(bv, wv, b_wv, h2, b_h2, KD)
                k = cnt["k"] % 2
                cnt["k"] += 1
                w = [pcol("ffw", l, q * KF + j) for q in range(3)]
                op("scalar", lambda e, k=k, bg=bg, w=w: e.activation(out=acc[k][:], in_=psum[bg][:], func=AF.Identity,
                                                                   scale=w[1]),
                   reads=[b_ps[bg], b_pvec], writes=[b_acc[k]])
                op("vector", lambda e, k=k, bg=bg, w=w: e.scalar_tensor_tensor(
                    out=acc[k][:, 1:TT], in0=psum[bg][:, 0:TT - 1], scalar=w[0], in1=acc[k][:, 1:TT],
                    op0=ALU.mult, op1=ALU.add), reads=[b_ps[bg], b_acc[k], b_pvec], writes=[b_acc[k]])
                op("vector", lambda e, k=k, bg=bg, w=w: e.scalar_tensor_tensor(
                    out=acc[k][:, 0:TT - 1], in0=psum[bg][:, 1:TT], scalar=w[2], in1=acc[k][:, 0:TT - 1],
                    op0=ALU.mult, op1=ALU.add), reads=[b_ps[bg], b_acc[k], b_pvec], writes=[b_acc[k]])
                op("vector", lambda e, k=k, j=j, w=w, i=i: e.scalar_tensor_tensor(
                    out=acc[k][:, 0:1], in0=ghl[:, j, i:i + 1], scalar=w[0], in1=acc[k][:, 0:1],
                    op0=ALU.mult, op1=ALU.add), reads=[b_ghl, b_acc[k], b_pvec], writes=[b_acc[k]])
                op("vector", lambda e, k=k, j=j, w=w, i=i: e.scalar_tensor_tensor(
                    out=acc[k][:, TT - 1:TT], in0=ghr[:, j, i:i + 1], scalar=w[2], in1=acc[k][:, TT - 1:TT],
                    op0=ALU.mult, op1=ALU.add), reads=[b_ghr, b_acc[k], b_pvec], writes=[b_acc[k]])
                op("scalar", lambda e, k=k: e.activation(out=sil[k][:], in_=acc[k][:], func=AF.Silu),
                   reads=[b_acc[k]], writes=[b_sil[k]])
                op("vector", lambda e, k=k, bv=bv, a_s=a_s, jj=jj: e.tensor_tensor(
                    out=aa[a_s][:, jj, :], in0=sil[k][:], in1=psum[bv][:], op=ALU.mult),
                   reads=[b_sil[k], b_ps[bv]], writes=[b_aa[a_s]])

        def down_group(i, gi):
            j0, gn = c.groups[gi]
            a_s = gi % 2
            for mq in range(MQ):
                wt, b_w = wd.load(wsrc(("dn", l, gi, mq), gn * 512), elems=gn * 512)
                for m4 in range(4):
                    m = mq * 4 + m4
                    bank = 4 + cnt["d"] % 3
                    cnt["d"] += 1

                    def f(e, wt=wt, m4=m4, bank=bank, a_s=a_s, gn=gn):
                        for jj in range(gn):
                            ins = e.matmul(psum[bank][:], lhsT=wt[:, jj * 512 + m4 * 128: jj * 512 + (m4 + 1) * 128],
                                           rhs=aa[a_s][:, jj, :], start=(jj == 0), stop=(jj == gn - 1))
                        return ins
                    op("tensor", f, reads=[b_w, b_aa[a_s]], writes=[b_ps[bank]])
                    op("vector", lambda e, m=m, bank=bank: e.tensor_tensor(out=xs[:, m, :], in0=xs[:, m, :],
                                                                         in1=psum[bank][:], op=ALU.add),
                       reads=[b_ps[bank], b_xs], writes=[b_xs])

        def load_xs(i, q):
            t0 = i * TT
            op(q, lambda e: e.dma_start(out=xs[:], in_=xT[:, :, t0:t0 + TT].rearrange("c p t -> p c t")),
               writes=[b_xs], dma=b_xs)

        def load_h2(i, q):
            t0 = i * TT
            op(q, lambda e: e.dma_start(out=h2[:], in_=h2T[:, :, t0:t0 + TT].rearrange("c p t -> p c t")),
               writes=[b_h2], dma=b_h2)

        ng = len(c.groups)
        for i in range(NT):
            t0 = i * TT
            if last or i == 0:
                load_xs(i, "gpsimd")
                load_h2(i, "sync" if i == 0 else "gpsimd")
            for gi in range(ng):
                if not last:
                    bg(spread(spread(c.NU // DEPTH, NT, i), ng, gi))
                up_group(i, gi)
                if gi >= 1:
                    down_group(i, gi - 1)
            if not last and i + 1 < NT:
                load_h2(i + 1, "sync")
            down_group(i, ng - 1)
            if not last:
                op("gpsimd", lambda e, t0=t0: e.dma_start(out=xT[:, :, t0:t0 + TT].rearrange("c p t -> p c t"),
                                                          in_=xs[:]), reads=[b_xs], dma=b_xs)
                if i + 1 < NT:
                    load_xs(i + 1, "gpsimd")
            else:
                rmsnorm_tile(xs, b_xs, "gfin", 0, xs, b_xs, sq, b_sq, rstd, b_rstd, 7)
                for ts in range(4):
                    o = cnt["o"] % 2
                    cnt["o"] += 1
                    for c4 in range(KD // 4):
                        bank = 4 + cnt["d"] % 3
                        cnt["d"] += 1

                        def tr(e, ts=ts, c4=c4, bank=bank):
                            for q in range(4):
                                cc = c4 * 4 + q
                                ins = e.transpose(out=psum[bank][:, q * 128:(q + 1) * 128],
                                                  in_=xs[:, cc, ts * 128:(ts + 1) * 128], identity=ident[:])
                            return ins
                        op("tensor", tr, reads=[b_xs, b_ident], writes=[b_ps[bank]])
                        eng = evac_eng()
                        op(eng, copy_op(eng, ost[o][:, c4 * 512:(c4 + 1) * 512], psum[bank][:]),
                           reads=[b_ps[bank]], writes=[b_ost[o], b_h2])
                    op("gpsimd", lambda e, o=o, t0=t0, ts=ts: e.dma_start(
                        out=y_out[t0 + ts * 128:t0 + (ts + 1) * 128, :], in_=ost[o][:]),
                       reads=[b_ost[o], b_h2], dma=b_ost[o])

    def run_phase(f, *a):
        P.phase_begin()
        f(*a)
        P.phase_end()
        conv["done"] = conv["issued"]

    def first_phase():
        prepass_w0()
        prepass_dft()

    run_phase(first_phase)
    run_phase(phase_T)
    for l in range(DEPTH):
        run_phase(phase_A, l)
        run_phase(phase_A2, l)
        run_phase(phase_B, l)
        run_phase(phase_C1, l)
        run_phase(phase_H, l)
        run_phase(phase_C2, l)

    with nc.Block() as block:
        P.emit(block)
    stack.close()
    return nc


def pack_pvec(cfg, inp, Bval):
    c = cfg
    pv = np.zeros((128, c.NPV), np.float32)

    def put(name, l, vec):
        n = vec.shape[0] // 128
        o = c.pv[(name, l)]
        pv[:, o:o + n] = vec.reshape(n, 128).T

    def putk(name, l, mat):
        K = mat.shape[0]
        n = mat.shape[1] // 128
        o = c.pv[(name, l)]
        pv[:, o:o + K * n] = mat.reshape(K, n, 128).transpose(2, 0, 1).reshape(128, K * n)

    for l in range(c.DEPTH):
        put("gmix", l, inp["norm_mix_g"][l])
        put("gffn", l, inp["norm_ffn_g"][l])
        putk("scw", l, inp["sc_conv_w"][l])
        putk("cfw", l, inp["cf_conv_w"][l])
        put("cfb", l, inp["cf_conv_b"][l])
        put("lng", l, inp["cf_ln_g"][l])
        put("lnb", l, inp["cf_ln_b"][l])
        putk("ffw", l, inp["ffn_conv_w"][l])
    put("gfin", 0, inp["final_norm_g"])
    pv[:, c.pv[("B", 0)]] = Bval
    return pv


def dft_factor_tables(cfg, connected):
    NTOK, NS = cfg.NTOK, cfg.NS
    S = NTOK if connected else cfg.SEG
    a = np.arange(NS, dtype=np.int64)
    k = np.arange(NTOK, dtype=np.int64)
    a_loc = a % (S // 128)
    k_loc = k % S
    same = ((a // (S // 128))[:, None] == (k // S)[None, :]).astype(np.float64)
    alpha = ((128 * a_loc[:, None] * k_loc[None, :]) % S).astype(np.float64) * (2.0 * np.pi / S)
    b = np.arange(128, dtype=np.int64)
    beta = ((b[:, None] * k_loc[None, :]) % S).astype(np.float64) * (2.0 * np.pi / S)
    sc = 1.0 / np.sqrt(S * 128.0)
    rowtab = np.zeros((2, 2, NS, NTOK), np.float32)
    rowtab[0, 0] = sc * np.cos(alpha) * same
    rowtab[0, 1] = -sc * np.sin(alpha) * same
    rowtab[1, 0] = -sc * np.sin(alpha) * same
    rowtab[1, 1] = -sc * np.cos(alpha) * same
    ptab = np.zeros((128, 2, NTOK), np.float32)
    ptab[:, 0, :] = np.cos(beta)
    ptab[:, 1, :] = np.sin(beta)
    return rowtab, ptab


def chan_dft():
    idx = (np.arange(128)[:, None] * np.arange(128)[None, :]) % 128
    ang = idx.astype(np.float64) * (2.0 * np.pi / 128)
    return np.concatenate([np.cos(ang), np.sin(ang)], axis=1).astype(np.float32)


def weight_unit(cfg, inp, key):
    c = cfg
    out = np.zeros((128, c.UE), np.float32)
    kind, l = key[0], key[1]
    if kind == "dn":
        gi, mq = key[2], key[3]
        j0, gn = c.groups[gi]
        blk = inp["w_down"][l][j0 * 128:(j0 + gn) * 128, mq * 512:(mq + 1) * 512]
        out[:, 0:gn * 512] = blk.reshape(gn, 128, 512).transpose(1, 0, 2).reshape(128, gn * 512)
        return out
    if kind == "in":
        src = c.win_units()[key[2]][2]
        blk = inp["w_in"][l][:, src * 128:(src + 1) * 128]
    elif kind == "out":
        m = key[2]
        blk = inp["w_out"][l][:, m * 128:(m + 1) * 128]
    else:
        j, h = key[2], key[3]
        col = h * c.DFF + j * 128
        blk = inp["w_up"][l][:, col:col + 128]
    out[:, 0:c.KD * 128] = blk.reshape(c.KD, 128, 128).transpose(1, 0, 2).reshape(128, c.KD * 128)
    return out


def weight_tiles(cfg, inp):
    c = cfg
    w = np.zeros((c.NU, 128, c.UE), np.float32)
    for g, key in enumerate(c.units):
        w[g] = weight_unit(c, inp, key)
    return w


_NC_CACHE = {}


def run_slots(cfg, slots, inp, trace=False):
    key = (cfg.D, cfg.NTOK, cfg.DEPTH)
    if key not in _NC_CACHE:
        _NC_CACHE[key] = build_program(cfg)
    nc = _NC_CACHE[key]
    tabs = {True: dft_factor_tables(cfg, True), False: dft_factor_tables(cfg, False)}
    pvs = {True: pack_pvec(cfg, inp, 1.0), False: pack_pvec(cfg, inp, 0.0)}
    cd = chan_dft()
    eye = np.eye(128, dtype=np.float32)
    wt = weight_tiles(cfg, inp)
    in_maps = []
    for r, (xs, conn) in enumerate(slots):
        in_maps.append({"x": xs, "wsh": wt, "pvec": pvs[conn], "rowtab": tabs[conn][0],
                        "ptab": tabs[conn][1], "cdft": cd, "ident": eye})
    res = run_bass_kernel_spmd(nc, in_maps, core_ids=list(range(cfg.NCORES)), trace=trace)
    return [r["y"] for r in res.results], res


def kernel(**inputs):
    cfg = Cfg()
    inp = {k: np.asarray(v) for k, v in inputs.items()}
    xp = np.ascontiguousarray(inp["x_prompt"], dtype=np.float32)
    xsm = np.ascontiguousarray(inp["x_sample"], dtype=np.float32)
    slots = [(xp[i], True) for i in range(4)]
    slots.append((xsm[0:2].reshape(cfg.NTOK, cfg.D), False))
    slots.append((xsm[2:4].reshape(cfg.NTOK, cfg.D), False))
    ys, _ = run_slots(cfg, slots, inp)
    y_prompt = np.stack(ys[0:4], axis=0)
    y_sample = np.concatenate([ys[4].reshape(2, cfg.SEG, cfg.D), ys[5].reshape(2, cfg.SEG, cfg.D)], axis=0)
    return (np.ascontiguousarray(y_prompt, dtype=np.float32), np.ascontiguousarray(y_sample, dtype=np.float32))
```
